# Optimizing a Trainium2 kernel written in Bass

```python
import math
import jax, jax.numpy as jnp
from jax import lax
import numpy as np

D_MODEL = 2048
BATCH = 1
SEQ = 8192
DEPTH = 4

CHUNK = 64
Q_BLOCK = 128
D_MIX = D_MODEL

MLA_HEADS = 6
MLA_NOPE = 128
MLA_ROPE = 64
MLA_V = 128
MLA_Q_RANK = 512
MLA_KV_RANK = 256
MLA_WIDTH = MLA_HEADS * MLA_V
ROPE_THETA = 10000.0

SG_GROUPS = 4
SG_GROUP_CH = 128
SG_WIDTH = SG_GROUPS * SG_GROUP_CH
SG_CHUNK = 128

SB_HEADS = 4
SB_HEAD_DIM = 128
SB_WIDTH = SB_HEADS * SB_HEAD_DIM

MEM_TOKENS = 256
MEM_HEADS = 4
MEM_HEAD_DIM = 64
MEM_WIDTH = MEM_HEADS * MEM_HEAD_DIM

IN_SIZES = (MLA_Q_RANK, MLA_KV_RANK, MLA_ROPE, MLA_WIDTH,
            SG_WIDTH, SG_WIDTH, SG_WIDTH,
            SB_WIDTH, SB_WIDTH, SB_WIDTH, SB_WIDTH,
            MEM_WIDTH, MEM_WIDTH)
D_IN = sum(IN_SIZES)

DEEPNORM_ALPHA = (2.0 * DEPTH) ** 0.25
DEEPNORM_BETA = (8.0 * DEPTH) ** -0.25
LN_EPS = 1e-5
RMS_EPS = 1e-6

kernel_name = "hybrid_mla_gmlp_stickbreak_deepnorm"


def _layer_norm(x, g, b):
    xf = x.astype(jnp.float32)
    mu = jnp.mean(xf, axis=-1, keepdims=True)
    xc = xf - mu
    var = jnp.mean(xc * xc, axis=-1, keepdims=True)
    return (xc * lax.rsqrt(var + LN_EPS) * g.astype(jnp.float32) + b.astype(jnp.float32)).astype(x.dtype)


def _rms_norm(x, g):
    xf = x.astype(jnp.float32)
    ms = jnp.mean(xf * xf, axis=-1, keepdims=True)
    return (xf * lax.rsqrt(ms + RMS_EPS) * g.astype(jnp.float32)).astype(x.dtype)


def _rope(x, cos, sin):
    half = x.shape[-1] // 2
    x1, x2 = x[..., :half], x[..., half:]
    return jnp.concatenate([x1 * cos - x2 * sin, x1 * sin + x2 * cos], axis=-1)


def _sweep_query_blocks(block_fn, q):
    b, s, h, d = q.shape
    nb = s // Q_BLOCK
    qb = q.reshape(b, nb, Q_BLOCK, h, d).transpose(1, 0, 2, 3, 4)
    out = lax.map(block_fn, (qb, jnp.arange(nb, dtype=jnp.int32)))
    return out.transpose(1, 0, 2, 3, 4).reshape(b, s, h, out.shape[-1])


def _chunk_causal_softmax_attention(q, k, v, scale):
    key_chunk = jnp.arange(k.shape[1]) // CHUNK

    def block(args):
        qblk, i = args
        qpos = i * Q_BLOCK + jnp.arange(Q_BLOCK)
        logits = jnp.einsum('bqhd,bkhd->bhqk', qblk, k).astype(jnp.float32) * scale
        mask = key_chunk[None, :] <= (qpos // CHUNK)[:, None]
        p = jax.nn.softmax(jnp.where(mask, logits, -jnp.inf), axis=-1)
        return jnp.einsum('bhqk,bkhd->bqhd', p.astype(v.dtype), v)

    return _sweep_query_blocks(block, q)


def _stick_breaking_attention(q, k, v, scale):
    kpos = jnp.arange(k.shape[1])

    def block(args):
        qblk, i = args
        qpos = i * Q_BLOCK + jnp.arange(Q_BLOCK)
        z = jnp.einsum('bqhd,bkhd->bhqk', qblk, k).astype(jnp.float32) * scale
        strict = kpos[None, :] < qpos[:, None]
        log_beta = jax.nn.log_sigmoid(z)
        log_1mb = jnp.where(strict, jax.nn.log_sigmoid(-z), 0.0)
        rev = lax.cumsum(log_1mb, axis=3, reverse=True)
        log_a = log_beta + rev - log_1mb
        a = jnp.where(strict, jnp.exp(log_a), 0.0)
        return jnp.einsum('bhqk,bkhd->bqhd', a.astype(v.dtype), v)

    return _sweep_query_blocks(block, q)


def setup_inputs(seed: int = 0) -> dict:
    key = jax.random.key(seed)
    ks = jax.random.split(key, 20)
    f32 = jnp.float32
    nrm = lambda k, shape, s: jax.random.normal(k, shape, f32) * s
    x = jax.random.normal(ks[0], (BATCH, SEQ, D_MODEL), f32)
    mem = jax.random.normal(ks[1], (BATCH, MEM_TOKENS, D_MODEL), f32)
    offset = jax.random.randint(ks[2], (BATCH, 1), 0, 4096, dtype=jnp.int32)
    positions = (offset + jnp.arange(SEQ, dtype=jnp.int32)[None, :]).astype(jnp.int32)
    w_in = nrm(ks[3], (DEPTH, D_MODEL, D_IN), D_MODEL ** -0.5)
    q_norm_g = 1.0 + nrm(ks[4], (DEPTH, MLA_Q_RANK), 0.01)
    w_uq = nrm(ks[5], (DEPTH, MLA_Q_RANK, MLA_HEADS * (MLA_NOPE + MLA_ROPE)), MLA_Q_RANK ** -0.5)
    kv_norm_g = 1.0 + nrm(ks[6], (DEPTH, MLA_KV_RANK), 0.01)
    w_ukv = nrm(ks[7], (DEPTH, MLA_KV_RANK, MLA_HEADS * (MLA_NOPE + MLA_V)), MLA_KV_RANK ** -0.5)
    sg_ln_g = 1.0 + nrm(ks[8], (DEPTH, SG_WIDTH), 0.01)
    sg_ln_b = nrm(ks[9], (DEPTH, SG_WIDTH), 0.01)
    sg_w = nrm(ks[10], (DEPTH, SG_GROUPS, SG_CHUNK, SG_CHUNK), SG_CHUNK ** -0.5)
    sg_b = 1.0 + nrm(ks[11], (DEPTH, SG_GROUPS, SG_CHUNK), 0.01)
    w_mem_k = nrm(ks[12], (DEPTH, D_MODEL, MEM_WIDTH), D_MODEL ** -0.5)
    w_mem_v = nrm(ks[13], (DEPTH, D_MODEL, MEM_WIDTH), D_MODEL ** -0.5)
    w_out = nrm(ks[14], (DEPTH, D_MIX, D_MODEL), D_MIX ** -0.5 * DEEPNORM_BETA)
    ln_g = 1.0 + nrm(ks[15], (DEPTH, D_MODEL), 0.01)
    ln_b = nrm(ks[16], (DEPTH, D_MODEL), 0.01)
    return {"x": x, "mem": mem, "positions": positions, "w_in": w_in,
            "q_norm_g": q_norm_g, "w_uq": w_uq, "kv_norm_g": kv_norm_g, "w_ukv": w_ukv,
            "sg_ln_g": sg_ln_g, "sg_ln_b": sg_ln_b, "sg_w": sg_w, "sg_b": sg_b,
            "w_mem_k": w_mem_k, "w_mem_v": w_mem_v, "w_out": w_out,
            "ln_g": ln_g, "ln_b": ln_b}


def reference(x, mem, positions, w_in, q_norm_g, w_uq, kv_norm_g, w_ukv,
              sg_ln_g, sg_ln_b, sg_w, sg_b, w_mem_k, w_mem_v, w_out, ln_g, ln_b):
    b, s, _ = x.shape
    inv_freq = ROPE_THETA ** (-jnp.arange(0, MLA_ROPE, 2, dtype=jnp.float32) / MLA_ROPE)
    ang = positions.astype(jnp.float32)[..., None] * inv_freq[None, None, :]
    cos = jnp.cos(ang).astype(x.dtype)
    sin = jnp.sin(ang).astype(x.dtype)
    split_idx = [int(v) for v in np.cumsum(IN_SIZES)[:-1]]
    p_in = jnp.arange(SG_CHUNK) // CHUNK
    sg_mask = (p_in[None, :] <= p_in[:, None]).astype(x.dtype)
    mla_scale = 1.0 / math.sqrt(MLA_NOPE + MLA_ROPE)
    sb_scale = 1.0 / math.sqrt(SB_HEAD_DIM)
    mem_scale = 1.0 / math.sqrt(MEM_HEAD_DIM)

    for l in range(DEPTH):
        h = jnp.einsum('bsd,de->bse', x, w_in[l])
        (c_q, c_kv, k_pe, g_a, sg_u, sg_v, g_b, sb_q, sb_k, sb_v, g_c, m_q, g_m) = jnp.split(h, split_idx, axis=-1)

        q = jnp.einsum('bsr,re->bse', _rms_norm(c_q, q_norm_g[l]), w_uq[l]).reshape(b, s, MLA_HEADS, MLA_NOPE + MLA_ROPE)
        q = jnp.concatenate([q[..., :MLA_NOPE], _rope(q[..., MLA_NOPE:], cos[:, :, None, :], sin[:, :, None, :])], axis=-1)
        kv = jnp.einsum('bsr,re->bse', _rms_norm(c_kv, kv_norm_g[l]), w_ukv[l]).reshape(b, s, MLA_HEADS, MLA_NOPE + MLA_V)
        k_rot = jnp.broadcast_to(_rope(k_pe, cos, sin)[:, :, None, :], (b, s, MLA_HEADS, MLA_ROPE))
        k = jnp.concatenate([kv[..., :MLA_NOPE], k_rot], axis=-1)
        o_a = _chunk_causal_softmax_attention(q, k, kv[..., MLA_NOPE:], mla_scale).reshape(b, s, MLA_WIDTH)

        u = jax.nn.gelu(sg_u)
        vn = _layer_norm(jax.nn.gelu(sg_v), sg_ln_g[l], sg_ln_b[l])
        vn = vn.reshape(b, s // SG_CHUNK, SG_CHUNK, SG_GROUPS, SG_GROUP_CH)
        w_sp = sg_w[l] * sg_mask[None]
        mixed = jnp.einsum('gts,bnsgc->bntgc', w_sp, vn) + sg_b[l].T[None, None, :, :, None]
        o_b = u * mixed.reshape(b, s, SG_WIDTH)

        o_c = _stick_breaking_attention(sb_q.reshape(b, s, SB_HEADS, SB_HEAD_DIM),
                                        sb_k.reshape(b, s, SB_HEADS, SB_HEAD_DIM),
                                        sb_v.reshape(b, s, SB_HEADS, SB_HEAD_DIM), sb_scale).reshape(b, s, SB_WIDTH)

        mk = jnp.einsum('bmd,de->bme', mem, w_mem_k[l]).reshape(b, MEM_TOKENS, MEM_HEADS, MEM_HEAD_DIM)
        mv = jnp.einsum('bmd,de->bme', mem, w_mem_v[l]).reshape(b, MEM_TOKENS, MEM_HEADS, MEM_HEAD_DIM)
        mq = m_q.reshape(b, s, MEM_HEADS, MEM_HEAD_DIM)
        mp = jax.nn.softmax(jnp.einsum('bshe,bmhe->bhsm', mq, mk).astype(jnp.float32) * mem_scale, axis=-1)
        o_m = jnp.einsum('bhsm,bmhe->bshe', mp.astype(mv.dtype), mv).reshape(b, s, MEM_WIDTH)

        y = jnp.concatenate([o_a * jax.nn.silu(g_a), o_b * jax.nn.silu(g_b),
                             o_c * jax.nn.silu(g_c), o_m * jax.nn.silu(g_m)], axis=-1)
        y = jnp.einsum('bse,ed->bsd', y, w_out[l])

        x = _layer_norm(DEEPNORM_ALPHA * x + y, ln_g[l], ln_b[l])
    return x
```

```python
import math
import numpy as np
import ml_dtypes
import concourse.bass as bass
import concourse.mybir as mybir
from concourse.bass_utils import run_bass_kernel_spmd

F32, BF16, I32 = mybir.dt.float32, mybir.dt.bfloat16, mybir.dt.int32
AF = mybir.ActivationFunctionType
ALU = mybir.AluOpType
AX = mybir.AxisListType

DEPTH = 4
D = 2048
DIN = 5696
NCORE = 8
KROWS = 6 * 128 + 64 + 4 * 128
VCOLS = 6 * 128 + 4 * 128
ALPHA = (2.0 * DEPTH) ** 0.25
MLA_SCALE = 1.0 / math.sqrt(192.0)
SB_SCALE = 1.0 / math.sqrt(128.0)
MEM_SCALE = 1.0 / math.sqrt(64.0)
TWO_PI = 2.0 * math.pi


class Sem:
    def __init__(self, h, step):
        self.h, self.step, self.n = h, step, 0


class T:
    def __init__(self, t=None, excl=False):
        self.t, self.w, self.r, self.excl = t, None, {}, excl


class Eng:
    def __init__(self, name, sem):
        self.name, self.sem, self.q, self.seen = name, sem, [], {}


class Ring:
    def __init__(self, items):
        self.items, self.i = items, 0

    def next(self):
        it = self.items[self.i % len(self.items)]
        self.i += 1
        return it


class K:
    def __init__(self, nc):
        self.nc = nc
        self.eng = {}
        for n in ("pe", "act", "dve", "pool", "sp"):
            self.eng[n] = Eng(n, Sem(nc.alloc_semaphore(name="s_" + n), 1))
        self.nsem = 0

    def dsem(self, step=16):
        self.nsem += 1
        return Sem(self.nc.alloc_semaphore(name="d%d" % self.nsem), step)

    SEM_LIMIT = 3000

    def op(self, en, fn, reads=(), writes=(), inc=True, sem=None, merge=False):
        if not getattr(self, 'enabled', True):
            return None
        e = self.eng[en]
        deps = []
        for t in reads:
            if t.w is not None:
                deps.extend(t.w.items())
            if t.excl:
                deps.extend(t.r.items())
        for t in writes:
            if t.w is not None:
                deps.extend(t.w.items())
            deps.extend(t.r.items())
        need = {}
        for (s, v) in deps:
            if en == "pe" and s is e.sem and sem is None:
                continue
            if e.seen.get(s, 0) < v:
                need[s] = max(need.get(s, 0), v)
        for s, v in need.items():
            e.q.append(("w", s, v))
            e.seen[s] = v
        if sem is None:
            s = e.sem
            if inc:
                s.n += 1
                ev = (s, s.n)
                e.q.append(("i", fn, s, 1))
                if s.n >= self.SEM_LIMIT:
                    self.nsem += 1
                    e.sem = Sem(self.nc.alloc_semaphore(name="s_%s_%d" % (en, self.nsem)), 1)
            else:
                ev = (s, s.n + 1)
                e.q.append(("i", fn, None, 0))
        else:
            sem.n += sem.step
            ev = (sem, sem.n)
            e.q.append(("i", fn, sem, sem.step))
        for t in writes:
            if merge and t.w is not None:
                t.w = dict(t.w)
                t.w[ev[0]] = max(t.w.get(ev[0], 0), ev[1])
                t.r = {}
            else:
                t.w, t.r = {ev[0]: ev[1]}, {}
        for t in reads:
            if ev[1] > t.r.get(ev[0], 0):
                t.r[ev[0]] = ev[1]
        return ev

    def emit(self):
        def run(e, h):
            for it in e.q:
                if it[0] == "w":
                    h.wait_ge(it[1].h, it[2])
                else:
                    ins = it[1](h)
                    if it[2] is not None:
                        if it[3] == 1:
                            ins.then_inc(it[2].h)
                        else:
                            ins.then_inc(it[2].h, it[3])
        with self.nc.Block() as block:
            @block.tensor
            def _(h):
                run(self.eng["pe"], h)

            @block.scalar
            def _(h):
                run(self.eng["act"], h)

            @block.vector
            def _(h):
                run(self.eng["dve"], h)

            @block.gpsimd
            def _(h):
                run(self.eng["pool"], h)

            @block.sync
            def _(h):
                run(self.eng["sp"], h)


GATE_CHUNKS = ((832, 0, 512), (1344, 512, 256), (2624, 768, 512), (4672, 1280, 512), (5440, 1792, 256))
C1_2PI = 6.28125
C2_2PI = TWO_PI - 6.28125


def build(depth=DEPTH, dbg=False, phases=None):
    PH = lambda name: phases is None or name in phases
    nc = bass.Bass("TRN2", target_bir_lowering=False)
    k = K(nc)

    def din(name, shape, dt=F32):
        return nc.dram_tensor(name, shape, dt, kind="ExternalInput").ap()

    x_d = din("x", [1024, D])
    pos_d = din("pos", [128, 8], I32)
    mem_d = din("mem", [256, D])
    w_in = din("w_in", [DEPTH, D, DIN])
    q_norm_g = din("q_norm_g", [DEPTH, 512])
    w_uq = din("w_uq", [DEPTH, 512, 1152])
    kv_norm_g = din("kv_norm_g", [DEPTH, 256])
    w_ukv = din("w_ukv", [DEPTH, 256, 1536])
    sg_ln_g = din("sg_ln_g", [DEPTH, 512])
    sg_ln_b = din("sg_ln_b", [DEPTH, 512])
    sg_w = din("sg_w", [DEPTH, 4, 128, 128])
    sg_b = din("sg_b", [DEPTH, 4, 128])
    w_mem_k = din("w_mem_k", [DEPTH, D, 256])
    w_mem_v = din("w_mem_v", [DEPTH, D, 256])
    w_out = din("w_out", [DEPTH, D, D])
    ln_g = din("ln_g", [DEPTH, D])
    ln_b = din("ln_b", [DEPTH, D])
    ident_d = din("ident", [128, 128], BF16)
    jrev_d = din("jrev", [128, 128], BF16)
    mmask_d = din("mmask", [128, 1024], BF16)
    sbm1_d = din("sbm1", [128, 1024], F32)
    invf_d = din("invf", [128, 32], F32)
    sgmask_d = din("sgmask", [128, 128], F32)
    out_d = nc.dram_tensor("out", [1024, D], F32, kind="ExternalOutput").ap()
    out_T = [T(out_d) for _ in range(8)]
    if dbg:
        ydbg_d = nc.dram_tensor("ydbg", [1024, D], BF16, kind="ExternalOutput").ap()
        ydbg_T = T(ydbg_d)
        dg_T = T(nc.dram_tensor("d_g", [2, 128, 512], F32, kind="ExternalOutput").ap())
        dp_T = T(nc.dram_tensor("d_p", [2, 128, 512], BF16, kind="ExternalOutput").ap())
        dk_T = T(nc.dram_tensor("d_k", [2, 128, 512], BF16, kind="ExternalOutput").ap())
        dq_T = T(nc.dram_tensor("d_q", [128, 128], BF16, kind="ExternalOutput").ap())
        de_T = T(nc.dram_tensor("d_e", [2, 128, 513], F32, kind="ExternalOutput").ap())

    xs_d = nc.dram_tensor("xs", [1024, D], F32, kind="Internal").ap()
    xs_T = [T(xs_d) for _ in range(8)]
    SHROWS = KROWS + VCOLS
    shk_T = T(nc.dram_tensor("sh", [SHROWS, 1024], BF16, kind="Internal").ap())
    shv_T = shk_T
    gak_T = T(nc.dram_tensor("ga", [NCORE * SHROWS, 1024], BF16, kind="Internal", addr_space="Shared").ap())
    gav_T = gak_T
    shv_ap = shk_T.t[KROWS:SHROWS, :].rearrange("r c -> (r c)").rearrange("(p v) -> p v", v=VCOLS)
    ccsem = k.dsem(1)

    def sb(name, shape, dt):
        return T(nc.alloc_sbuf_tensor("sb_" + name, shape, dt))

    def sbr(name, shape, dt, n=2):
        return Ring([sb("%s%d" % (name, j), shape, dt) for j in range(n)])

    banks = [T(nc.alloc_psum_tensor("ps%d" % j, [128, 512], F32), excl=True) for j in range(8)]
    mmring = Ring(banks[0:2])
    trring = Ring(banks[2:4])
    accsets = Ring([(banks[4], banks[5]), (banks[6], banks[7])])
    smring = Ring(banks[4:8])

    def tsem(t):
        if not hasattr(t, "sem"):
            t.sem = k.dsem()
        return t.sem

    def load(dst, dst_ap, src_ap, q="sp", src_T=None, **kw):
        k.op(q, lambda h: h.dma_start(out=dst_ap, in_=src_ap, **kw), reads=[src_T] if src_T is not None else [],
             writes=[dst], sem=tsem(dst))

    def store(dst_T, dst_ap, src, src_ap, q="sp"):
        if not hasattr(src, "ssem"):
            src.ssem = k.dsem()
        k.op(q, lambda h: h.dma_start(out=dst_ap, in_=src_ap), reads=[src], writes=[dst_T], sem=src.ssem, merge=True)

    ident = sb("ident", [128, 128], BF16)
    jrev = sb("jrev", [128, 128], BF16)
    mmask = sb("mmask", [128, 1024], BF16)
    sbm1 = sb("sbm1", [128, 1024], F32)
    zeros = sb("zeros", [128, 512], F32)
    sgmask = sb("sgmask", [128, 128], F32)
    cosT = sb("cosT", [128, 8, 32], F32)
    sinT = sb("sinT", [128, 8, 32], F32)
    eps6 = sb("eps6", [128, 1], F32)
    eps5 = sb("eps5", [128, 1], F32)
    xT = sb("xT", [128, 16, 1024], BF16)
    ybuf = [sb("ybuf%d" % i, [128, D], BF16) for i in range(8)]
    qTn = sb("qTn", [128, 6, 1024], BF16)
    qTr = sb("qTr", [64, 6, 1024], BF16)
    sbqT = sb("sbqT", [128, 4, 1024], BF16)
    wring = sbr("wr", [128, 16, 256], BF16, 3)
    wsmall = sb("wsmall", [128, 4608], BF16)
    mkT = sb("mkT", [64, 4, 256], BF16)
    mv = sb("mv", [128, 2, 256], BF16)
    wspT = sb("wspT", [128, 4, 128], BF16)
    sgbT = sb("sgbT", [128, 4], F32)
    qg_bc = sb("qg_bc", [128, 512], F32)
    kvg_bc = sb("kvg_bc", [128, 256], F32)
    sgg_bc = sb("sgg_bc", [128, 512], F32)
    sgb_bc = sb("sgb_bc", [128, 512], F32)
    lnring = sbr("lnbc", [128, 2, 256], F32, 2)
    f32_r = sbr("f32r", [128, 512], F32, 3)
    bf_r = sbr("bfr", [128, 512], BF16, 4)
    xin_r = hseg_r = g_r = f32_r
    xb_r = hb_r = p_r = pf_r = bf_r
    st_r = sbr("st", [128, 8], F32, 4)
    cqnT = sb("cqnT", [128, 4, 128], BF16)
    kvnT = sb("kvnT", [128, 2, 128], BF16)
    qfr = sb("qfr", [128, 6, 64], F32)
    qb = sb("qb", [128, 1152], BF16)
    rt_r = sbr("rt", [128, 4, 32], F32, 2)
    kvb = sb("kvb", [128, 1536], BF16)
    kpeb = sb("kpeb", [128, 64], BF16)
    kst_r = sbr("kst", [128, 4, 128], BF16, 2)
    kpst = sb("kpst", [64, 128], BF16)
    vst_r = sbr("vst", [128, 768], BF16, 2)
    mqT = sb("mqT", [64, 4, 128], BF16)
    pm_r = sbr("pm", [128, 256], BF16, 2)
    pmT_r = sbr("pmT", [128, 2, 128], BF16, 2)
    kc_r = sbr("kc", [128, 4, 128], BF16, 3)
    kr_r = sbr("kr", [64, 4, 128], BF16, 3)
    vc_r = sbr("vc", [128, 4, 128], BF16, 3)
    e_r = sbr("eb", [128, 513], F32, 2)
    pt_r = sbr("ptb", [128, 4, 128], BF16, 2)
    rs_r = sbr("rs", [128, 8, 16], F32, 2)
    cry_r = sbr("cry", [128, 8], F32, 2)
    rsum = sb("rsum", [128, 8], F32)
    lnsum = sb("lnsum", [128, 8, 8], F32)
    lnsq = sb("lnsq", [128, 8, 8], F32)
    lnst = sb("lnst", [128, 8, 4], F32)
    zp_r = sbr("zp", [128, 256], F32, 3)
    zq_r = sbr("zq", [128, 256], F32, 1)

    load(ident, ident.t[:, :], ident_d[:, :])
    load(jrev, jrev.t[:, :], jrev_d[:, :])
    load(mmask, mmask.t[:, :], mmask_d[:, :])
    load(sbm1, sbm1.t[:, :], sbm1_d[:, :])
    load(sgmask, sgmask.t[:, :], sgmask_d[:, :])
    k.op("dve", lambda h: h.memset(zeros.t[:, :], 0.0), writes=[zeros])
    k.op("dve", lambda h: h.memset(eps6.t[:, :], 1e-6), writes=[eps6])
    k.op("dve", lambda h: h.memset(eps5.t[:, :], 1e-5), writes=[eps5])

    posi = sb("posi", [128, 8], I32)
    posf = sb("posf", [128, 8], F32)
    invf = sb("invf", [128, 32], F32)
    class _View:
        pass

    def view3(tile):
        v = T(tile.t[:, 0:256].rearrange("p (a b) -> p a b", a=8))
        return v
    _a, _b, _c = f32_r.next(), f32_r.next(), f32_r.next()
    ang, tq, tkf = view3(_a), view3(_b), view3(_c)
    tki = sb("tki", [128, 8, 32], I32)
    load(invf, invf.t[:, :], invf_d[:, :])
    load(posi, posi.t[:, :], pos_d[:, :])
    k.op("dve", lambda h: h.tensor_copy(out=posf.t[:, :], in_=posi.t[:, :]), reads=[posi], writes=[posf])
    for i in range(8):
        k.op("dve", lambda h, i=i: h.tensor_scalar(out=ang.t[:, i, :], in0=invf.t[:, :], scalar1=posf.t[:, i:i + 1],
                                                   scalar2=None, op0=ALU.mult), reads=[invf, posf], writes=[ang])
    k.op("dve", lambda h: h.tensor_scalar(out=tki.t[:, :, :], in0=ang.t[:, :, :], scalar1=1.0 / TWO_PI, scalar2=None, op0=ALU.mult),
         reads=[ang], writes=[tki])
    k.op("dve", lambda h: h.tensor_copy(out=tkf.t[:, :, :], in_=tki.t[:, :, :]), reads=[tki], writes=[tkf])
    k.op("dve", lambda h: h.scalar_tensor_tensor(out=ang.t[:, :, :], in0=tkf.t[:, :, :], scalar=-C1_2PI, in1=ang.t[:, :, :],
                                                 op0=ALU.mult, op1=ALU.add), reads=[tkf, ang], writes=[ang])
    k.op("dve", lambda h: h.scalar_tensor_tensor(out=ang.t[:, :, :], in0=tkf.t[:, :, :], scalar=-C2_2PI, in1=ang.t[:, :, :],
                                                 op0=ALU.mult, op1=ALU.add), reads=[tkf, ang], writes=[ang])
    k.op("dve", lambda h: h.tensor_scalar(out=tq.t[:, :, :], in0=ang.t[:, :, :], scalar1=3.14159, scalar2=-3.14159, op0=ALU.min, op1=ALU.max),
         reads=[ang], writes=[tq])
    k.op("act", lambda h: h.activation(out=sinT.t[:, :, :], in_=tq.t[:, :, :], func=AF.Sin), reads=[tq], writes=[sinT])
    k.op("dve", lambda h: h.tensor_scalar(out=ang.t[:, :, :], in0=ang.t[:, :, :], scalar1=0.5 * math.pi, scalar2=None, op0=ALU.add),
         reads=[ang], writes=[ang])
    k.op("dve", lambda h: h.tensor_scalar(out=tkf.t[:, :, :], in0=ang.t[:, :, :], scalar1=math.pi, scalar2=-TWO_PI, op0=ALU.is_gt, op1=ALU.mult),
         reads=[ang], writes=[tkf])
    k.op("dve", lambda h: h.tensor_tensor(out=ang.t[:, :, :], in0=ang.t[:, :, :], in1=tkf.t[:, :, :], op=ALU.add), reads=[ang, tkf], writes=[ang])
    k.op("dve", lambda h: h.tensor_scalar(out=tq.t[:, :, :], in0=ang.t[:, :, :], scalar1=3.14159, scalar2=-3.14159, op0=ALU.min, op1=ALU.max),
         reads=[ang], writes=[tq])
    k.op("act", lambda h: h.activation(out=cosT.t[:, :, :], in_=tq.t[:, :, :], func=AF.Sin), reads=[tq], writes=[cosT])
    for (ring_t, v) in ((_a, ang), (_b, tq), (_c, tkf)):
        ring_t.w, ring_t.r = (dict(v.w) if v.w else None), dict(v.r)

    def transposes(dst, dst_ap, src, src_aps, rev=False, evac="dve", m=128):
        bank = trring.next()
        n = len(src_aps)
        idm = jrev if rev else ident
        for j, sap in enumerate(src_aps):
            k.op("pe", lambda h, j=j, sap=sap: h.matmul(bank.t[0:m, j * 128:(j + 1) * 128], sap, idm.t[:, :], start=True, stop=True),
                 reads=[src, idm], writes=[bank], inc=(j == n - 1))
        src_ap = bank.t[0:m, 0:n * 128]
        if len(dst_ap.shape) == 3:
            src_ap = src_ap.rearrange("p (a b) -> p a b", a=n)
        if evac == "dve":
            k.op("dve", lambda h: h.tensor_copy(out=dst_ap, in_=src_ap), reads=[bank], writes=[dst])
        else:
            k.op("act", lambda h: h.copy(out=dst_ap, in_=src_ap), reads=[bank], writes=[dst])

    def wload(l_ap, width):
        w = wring.next()
        for q4 in range(4):
            load(w, w.t[:, q4 * 4:(q4 + 1) * 4, 0:width], l_ap[q4 * 512:(q4 + 1) * 512, :].rearrange("(kc p) e -> p kc e", p=128), q="pool")
        return w

    def rope(src, s0, s1, i, dst, d0, d1, eng="dve"):
        rt = rt_r.next()
        c, s = cosT.t[:, i, :], sinT.t[:, i, :]
        k.op(eng, lambda h: h.tensor_tensor(out=rt.t[:, 0, :], in0=s0, in1=c, op=ALU.mult), reads=[src, cosT], writes=[rt])
        k.op(eng, lambda h: h.tensor_tensor(out=rt.t[:, 1, :], in0=s1, in1=s, op=ALU.mult), reads=[src, sinT], writes=[rt])
        k.op(eng, lambda h: h.tensor_tensor(out=rt.t[:, 2, :], in0=s0, in1=s, op=ALU.mult), reads=[src, sinT], writes=[rt])
        k.op(eng, lambda h: h.tensor_tensor(out=rt.t[:, 3, :], in0=s1, in1=c, op=ALU.mult), reads=[src, cosT], writes=[rt])
        k.op(eng, lambda h: h.tensor_tensor(out=d0, in0=rt.t[:, 0, :], in1=rt.t[:, 1, :], op=ALU.subtract), reads=[rt], writes=[dst])
        k.op(eng, lambda h: h.tensor_tensor(out=d1, in0=rt.t[:, 2, :], in1=rt.t[:, 3, :], op=ALU.add), reads=[rt], writes=[dst])

    def rstd(st, col, n, eps_t):
        k.op("act", lambda h: h.activation(out=st.t[:, col:col + 1], in_=st.t[:, col:col + 1], func=AF.Sqrt, bias=eps_t.t[:, 0:1], scale=1.0 / n),
             reads=[st, eps_t], writes=[st])
        k.op("dve", lambda h: h.reciprocal(out=st.t[:, col:col + 1], in_=st.t[:, col:col + 1]), reads=[st], writes=[st])

    def bcv(dst, dap, vec):
        load(dst, dap, vec.partition_broadcast(128))

    for l in range(depth):
        xsrc = x_d if l == 0 else xs_d
        xsrc_T = [None] * 8 if l == 0 else xs_T

        k.enabled = PH("params")
        bcv(qg_bc, qg_bc.t[:, :], q_norm_g[l])
        bcv(kvg_bc, kvg_bc.t[:, :], kv_norm_g[l])
        bcv(sgg_bc, sgg_bc.t[:, :], sg_ln_g[l])
        bcv(sgb_bc, sgb_bc.t[:, :], sg_ln_b[l])
        load(sgbT, sgbT.t[:, :], sg_b[l].rearrange("g t -> t g"), allow_slow_non_contiguous=True)
        for g in range(4):
            xin = xin_r.next()
            load(xin, xin.t[:, 0:128], sg_w[l, g])
            k.op("dve", lambda h, xin=xin: h.tensor_tensor(out=xin.t[:, 0:128], in0=xin.t[:, 0:128], in1=sgmask.t[:, :], op=ALU.mult),
                 reads=[xin, sgmask], writes=[xin])
            xb = xb_r.next()
            k.op("dve", lambda h, xb=xb, xin=xin: h.tensor_copy(out=xb.t[:, 0:128], in_=xin.t[:, 0:128]), reads=[xin], writes=[xb])
            transposes(wspT, wspT.t[:, g, :], xb, [xb.t[:, 0:128]])
        memT = wring.next()
        for mb in range(2):
            for pc in range(4):
                xin = xin_r.next()
                load(xin, xin.t[:, :], mem_d[mb * 128:(mb + 1) * 128, pc * 512:(pc + 1) * 512])
                xb = xb_r.next()
                k.op("dve", lambda h, xb=xb, xin=xin: h.tensor_copy(out=xb.t[:, :], in_=xin.t[:, :]), reads=[xin], writes=[xb])
                transposes(memT, memT.t[:, pc * 4:(pc + 1) * 4, mb * 128:(mb + 1) * 128], xb, [xb.t[:, j * 128:(j + 1) * 128] for j in range(4)])
        w = wload(w_mem_k[l], 256)
        for hh in range(4):
            bank = smring.next()
            for kc in range(16):
                k.op("pe", lambda h, kc=kc, hh=hh, bank=bank, w=w, memT=memT: h.matmul(bank.t[0:64, 0:256], w.t[:, kc, hh * 64:(hh + 1) * 64], memT.t[:, kc, :],
                                                                      start=(kc == 0), stop=(kc == 15)), reads=[w, memT], writes=[bank], inc=(kc == 15))
            k.op("dve", lambda h, hh=hh, bank=bank: h.tensor_copy(out=mkT.t[:, hh, :], in_=bank.t[0:64, 0:256]), reads=[bank], writes=[mkT])
        w = wload(w_mem_v[l], 256)
        for mb in range(2):
            bank = smring.next()
            for kc in range(16):
                k.op("pe", lambda h, kc=kc, mb=mb, bank=bank, w=w, memT=memT: h.matmul(bank.t[:, 0:256], memT.t[:, kc, mb * 128:(mb + 1) * 128], w.t[:, kc, 0:256],
                                                                      start=(kc == 0), stop=(kc == 15)), reads=[w, memT], writes=[bank], inc=(kc == 15))
            k.op("dve", lambda h, mb=mb, bank=bank: h.tensor_copy(out=mv.t[:, mb, :], in_=bank.t[:, 0:256]), reads=[bank], writes=[mv])

        k.enabled = PH("xT")
        for i in range(8):
            for pc in range(4):
                xin = xin_r.next()
                load(xin, xin.t[:, :], xsrc[i * 128:(i + 1) * 128, pc * 512:(pc + 1) * 512], src_T=xsrc_T[i])
                xb = xb_r.next()
                k.op("act", lambda h, xb=xb, xin=xin: h.copy(out=xb.t[:, :], in_=xin.t[:, :]), reads=[xin], writes=[xb])
                transposes(xT, xT.t[:, pc * 4:(pc + 1) * 4, i * 128:(i + 1) * 128], xb, [xb.t[:, j * 128:(j + 1) * 128] for j in range(4)],
                           evac=("dve" if pc % 2 == 0 else "act"))

        def inproj(c0, width, handler):
            pieces = [(p0, min(256, width - p0)) for p0 in range(0, width, 256)]
            ws = [wload(w_in[l, :, c0 + p0:c0 + p0 + pw], pw) for (p0, pw) in pieces]
            pend = None
            for i in range(8):
                bank = mmring.next()
                for pi, (p0, pw) in enumerate(pieces):
                    for kc in range(16):
                        k.op("pe", lambda h, kc=kc, i=i, bank=bank, w=ws[pi], p0=p0, pw=pw: h.matmul(
                            bank.t[:, p0:p0 + pw], xT.t[:, kc, i * 128:(i + 1) * 128], w.t[:, kc, 0:pw], start=(kc == 0), stop=(kc == 15)),
                            reads=[xT, ws[pi]], writes=[bank], inc=(kc == 15 and pi == len(pieces) - 1))
                if pend is not None:
                    handler(*pend)
                pend = (i, bank)
            handler(*pend)

        k.enabled = PH("gates")
        for (c0, ycol, width) in GATE_CHUNKS:
            def gate(i, bank, ycol=ycol, width=width):
                k.op("act", lambda h: h.activation(out=ybuf[i].t[:, ycol:ycol + width], in_=bank.t[:, 0:width], func=AF.Silu),
                     reads=[bank], writes=[ybuf[i]])
            inproj(c0, width, gate)

        k.enabled = PH("kv")
        load(wsmall, wsmall.t[:, 0:3072].rearrange("p (a b) -> p a b", a=2), w_ukv[l].rearrange("(kc p) e -> p kc e", p=128), q="pool")

        def kv_handler(i, bank):
            ip = 7 - i
            hs = hseg_r.next()
            st = st_r.next()
            hb = hb_r.next()
            k.op("act", lambda h: h.copy(out=hs.t[:, 0:320], in_=bank.t[:, 0:320]), reads=[bank], writes=[hs])
            k.op("act", lambda h: h.activation(out=hb.t[:, 0:256], in_=hs.t[:, 0:256], func=AF.Square, accum_out=st.t[:, 0:1]),
                 reads=[hs], writes=[hb, st])
            rstd(st, 0, 256.0, eps6)
            k.op("dve", lambda h: h.scalar_tensor_tensor(out=hb.t[:, 0:256], in0=hs.t[:, 0:256], scalar=st.t[:, 0:1], in1=kvg_bc.t[:, :],
                                                         op0=ALU.mult, op1=ALU.mult), reads=[hs, st, kvg_bc], writes=[hb])
            transposes(kvnT, kvnT.t[:, :, :], hb, [hb.t[:, 0:128], hb.t[:, 128:256]])
            for n3 in range(3):
                bk = smring.next()
                for kc in range(2):
                    k.op("pe", lambda h, kc=kc, n3=n3, bk=bk: h.matmul(bk.t[:, :], kvnT.t[:, kc, :], wsmall.t[:, kc * 1536 + n3 * 512:kc * 1536 + (n3 + 1) * 512],
                                                                     start=(kc == 0), stop=(kc == 1)), reads=[kvnT, wsmall], writes=[bk], inc=(kc == 1))
                k.op("act", lambda h, n3=n3, bk=bk: h.copy(out=kvb.t[:, n3 * 512:(n3 + 1) * 512], in_=bk.t[:, :]), reads=[bk], writes=[kvb])
            for hp in range(2):
                ks = kst_r.next()
                nh = 4 if hp == 0 else 2
                transposes(ks, ks.t[:, 0:nh, :], kvb, [kvb.t[:, (hp * 4 + j) * 256:(hp * 4 + j) * 256 + 128] for j in range(nh)], rev=True)
                store(shk_T, shk_T.t[hp * 512:hp * 512 + nh * 128, ip * 128:(ip + 1) * 128].rearrange("(a p) c -> p a c", p=128), ks, ks.t[:, 0:nh, :])
            vs = vst_r.next()
            kv3 = kvb.t[:, :].rearrange("p (a b) -> p a b", b=256)
            for hp in range(2):
                bk = smring.next()
                k.op("pe", lambda h, hp=hp, bk=bk: h.matmul(bk.t[:, 0:384].rearrange("p (a b) -> p a b", a=3), jrev.t[:, :],
                                                            kv3[:, hp * 3:(hp + 1) * 3, 128:256], start=True, stop=True), reads=[kvb, jrev], writes=[bk])
                k.op("dve", lambda h, hp=hp, bk=bk: h.tensor_copy(out=vs.t[:, hp * 384:(hp + 1) * 384], in_=bk.t[:, 0:384]), reads=[bk], writes=[vs])
            store(shv_T, shv_ap[ip * 128:(ip + 1) * 128, 0:768], vs, vs.t[:, :])
            rope(hs, hs.t[:, 256:288], hs.t[:, 288:320], i, kpeb, kpeb.t[:, 0:32], kpeb.t[:, 32:64])
            transposes(kpst, kpst.t[:, :], kpeb, [kpeb.t[:, :]], rev=True, m=64)
            store(shk_T, shk_T.t[768:832, ip * 128:(ip + 1) * 128], kpst, kpst.t[:, :])
        inproj(512, 320, kv_handler)

        def sbk(i, bank):
            ip = 7 - i
            hb = hb_r.next()
            k.op("act", lambda h: h.copy(out=hb.t[:, :], in_=bank.t[:, :]), reads=[bank], writes=[hb])
            ks = kst_r.next()
            transposes(ks, ks.t[:, :, :], hb, [hb.t[:, j * 128:(j + 1) * 128] for j in range(4)], rev=True)
            store(shk_T, shk_T.t[832:1344, ip * 128:(ip + 1) * 128].rearrange("(a p) c -> p a c", p=128), ks, ks.t[:, :, :])
        inproj(3648, 512, sbk)

        def sbv(i, bank):
            ip = 7 - i
            hb = hb_r.next()
            k.op("act", lambda h: h.copy(out=hb.t[:, :], in_=bank.t[:, :]), reads=[bank], writes=[hb])
            bk = smring.next()
            k.op("pe", lambda h: h.matmul(bk.t[:, :], jrev.t[:, :], hb.t[:, :], start=True, stop=True), reads=[hb, jrev], writes=[bk])
            vs = vst_r.next()
            k.op("dve", lambda h: h.tensor_copy(out=vs.t[:, 0:512], in_=bk.t[:, :]), reads=[bk], writes=[vs])
            store(shv_T, shv_ap[ip * 128:(ip + 1) * 128, 768:1280], vs, vs.t[:, 0:512])
        inproj(4160, 512, sbv)

        k.enabled = PH("exch")
        k.op("pool", lambda h: h.collective_compute("AllGather", ALU.bypass, replica_groups=[list(range(NCORE))],
                                                    ins=[shk_T.t[:, :]], outs=[gak_T.t[:, :]]), reads=[shk_T], writes=[gak_T], sem=ccsem)

        k.enabled = PH("q")
        load(wsmall, wsmall.t[:, 0:4608].rearrange("p (a b) -> p a b", a=4), w_uq[l].rearrange("(kc p) e -> p kc e", p=128), q="pool")

        def cq_handler(i, bank):
            hs = hseg_r.next()
            st = st_r.next()
            hb = hb_r.next()
            k.op("act", lambda h: h.copy(out=hs.t[:, :], in_=bank.t[:, :]), reads=[bank], writes=[hs])
            k.op("act", lambda h: h.activation(out=hb.t[:, :], in_=hs.t[:, :], func=AF.Square, accum_out=st.t[:, 0:1]), reads=[hs], writes=[hb, st])
            rstd(st, 0, 512.0, eps6)
            k.op("dve", lambda h: h.scalar_tensor_tensor(out=hb.t[:, :], in0=hs.t[:, :], scalar=st.t[:, 0:1], in1=qg_bc.t[:, :],
                                                         op0=ALU.mult, op1=ALU.mult), reads=[hs, st, qg_bc], writes=[hb])
            transposes(cqnT, cqnT.t[:, :, :], hb, [hb.t[:, j * 128:(j + 1) * 128] for j in range(4)])
            for n3 in range(3):
                bk = smring.next()
                for kc in range(4):
                    k.op("pe", lambda h, kc=kc, n3=n3, bk=bk: h.matmul(bk.t[:, 0:384], cqnT.t[:, kc, :], wsmall.t[:, kc * 1152 + n3 * 384:kc * 1152 + (n3 + 1) * 384],
                                                                     start=(kc == 0), stop=(kc == 3)), reads=[cqnT, wsmall], writes=[bk], inc=(kc == 3))
                k.op("act", lambda h, n3=n3, bk=bk: h.activation(out=qb.t[:, n3 * 384:(n3 + 1) * 384], in_=bk.t[:, 0:384], func=AF.Copy, scale=MLA_SCALE),
                     reads=[bk], writes=[qb])
                k.op("dve", lambda h, n3=n3, bk=bk: h.tensor_scalar(out=qfr.t[:, 2 * n3:2 * n3 + 2, :],
                                                                   in0=bk.t[:, 0:384].rearrange("p (a b) -> p a b", a=2)[:, :, 128:192],
                                                                   scalar1=MLA_SCALE, scalar2=None, op0=ALU.mult), reads=[bk], writes=[qfr])
            for hh in range(6):
                b0 = hh * 192 + 128
                rope(qfr, qfr.t[:, hh, 0:32], qfr.t[:, hh, 32:64], i, qb, qb.t[:, b0:b0 + 32], qb.t[:, b0 + 32:b0 + 64],
                     eng="dve")
            for g4 in (0, 4):
                nh = 4 if g4 == 0 else 2
                transposes(qTn, qTn.t[:, g4:g4 + nh, i * 128:(i + 1) * 128], qb, [qb.t[:, (g4 + j) * 192:(g4 + j) * 192 + 128] for j in range(nh)])
                transposes(qTr, qTr.t[:, g4:g4 + nh, i * 128:(i + 1) * 128], qb, [qb.t[:, (g4 + j) * 192 + 128:(g4 + j) * 192 + 192] for j in range(nh)],
                           m=64, evac="act")
        inproj(0, 512, cq_handler)

        k.enabled = PH("sbq")
        def sbq(i, bank):
            hb = hb_r.next()
            k.op("act", lambda h: h.activation(out=hb.t[:, :], in_=bank.t[:, :], func=AF.Copy, scale=SB_SCALE), reads=[bank], writes=[hb])
            transposes(sbqT, sbqT.t[:, :, i * 128:(i + 1) * 128], hb, [hb.t[:, j * 128:(j + 1) * 128] for j in range(4)])
        inproj(3136, 512, sbq)

        k.enabled = PH("mem")
        def mqh(i, bank):
            hb = hb_r.next()
            st = st_r.next()
            k.op("act", lambda h: h.activation(out=hb.t[:, 0:256], in_=bank.t[:, 0:256], func=AF.Copy, scale=MEM_SCALE), reads=[bank], writes=[hb])
            transposes(mqT, mqT.t[:, :, :], hb, [hb.t[:, j * 64:(j + 1) * 64] for j in range(4)], m=64)
            obank = smring.next()
            sbanks = (smring.next(), smring.next())
            for hh in range(4):
                sbk_ = sbanks[hh // 2]
                sc0 = (hh % 2) * 256
                k.op("pe", lambda h, hh=hh, sbk_=sbk_, sc0=sc0: h.matmul(sbk_.t[:, sc0:sc0 + 256], mqT.t[:, hh, :], mkT.t[:, hh, :], start=True, stop=True),
                     reads=[mqT, mkT], writes=[sbk_])
                pm = pm_r.next()
                k.op("act", lambda h, hh=hh, sbk_=sbk_, pm=pm, sc0=sc0: h.activation(out=pm.t[:, :], in_=sbk_.t[:, sc0:sc0 + 256], func=AF.Exp,
                                                                                     accum_out=st.t[:, hh:hh + 1]),
                     reads=[sbk_], writes=[pm, st])
                pmT = pmT_r.next()
                transposes(pmT, pmT.t[:, :, :], pm, [pm.t[:, 0:128], pm.t[:, 128:256]])
                for mb in range(2):
                    k.op("pe", lambda h, hh=hh, mb=mb, pmT=pmT: h.matmul(obank.t[:, hh * 64:(hh + 1) * 64], pmT.t[:, mb, :], mv.t[:, mb, hh * 64:(hh + 1) * 64],
                                                                         start=(mb == 0), stop=(mb == 1)), reads=[pmT, mv], writes=[obank], inc=(mb == 1))
            k.op("dve", lambda h: h.reciprocal(out=st.t[:, 4:8], in_=st.t[:, 0:4]), reads=[st], writes=[st])
            for hh in range(4):
                c0 = 1792 + hh * 64
                k.op("dve", lambda h, hh=hh, c0=c0: h.scalar_tensor_tensor(out=ybuf[i].t[:, c0:c0 + 64], in0=obank.t[:, hh * 64:(hh + 1) * 64],
                                                                           scalar=st.t[:, 4 + hh:5 + hh], in1=ybuf[i].t[:, c0:c0 + 64],
                                                                           op0=ALU.mult, op1=ALU.mult), reads=[obank, st, ybuf[i]], writes=[ybuf[i]])
        inproj(5184, 256, mqh)

        k.enabled = PH("gmlp")
        def sgv(i, bank):
            hs = hseg_r.next()
            st = st_r.next()
            hb = hb_r.next()
            k.op("act", lambda h: h.activation(out=hs.t[:, :], in_=bank.t[:, :], func=AF.Gelu_apprx_tanh, accum_out=st.t[:, 0:1]), reads=[bank], writes=[hs, st])
            k.op("dve", lambda h: h.tensor_scalar(out=st.t[:, 0:1], in0=st.t[:, 0:1], scalar1=1.0 / 512.0, scalar2=None, op0=ALU.mult), reads=[st], writes=[st])
            k.op("dve", lambda h: h.tensor_scalar(out=hs.t[:, :], in0=hs.t[:, :], scalar1=st.t[:, 0:1], scalar2=None, op0=ALU.subtract), reads=[hs, st], writes=[hs])
            k.op("act", lambda h: h.activation(out=hb.t[:, :], in_=hs.t[:, :], func=AF.Square, accum_out=st.t[:, 1:2]), reads=[hs], writes=[hb, st])
            rstd(st, 1, 512.0, eps5)
            k.op("dve", lambda h: h.scalar_tensor_tensor(out=hs.t[:, :], in0=hs.t[:, :], scalar=st.t[:, 1:2], in1=sgg_bc.t[:, :], op0=ALU.mult, op1=ALU.mult),
                 reads=[hs, st, sgg_bc], writes=[hs])
            k.op("dve", lambda h: h.tensor_tensor(out=hb.t[:, :], in0=hs.t[:, :], in1=sgb_bc.t[:, :], op=ALU.add), reads=[hs, sgb_bc], writes=[hb])
            bk = smring.next()
            for g in range(4):
                k.op("pe", lambda h, g=g: h.matmul(bk.t[:, g * 128:(g + 1) * 128], wspT.t[:, g, :], hb.t[:, g * 128:(g + 1) * 128], start=True, stop=True),
                     reads=[wspT, hb], writes=[bk], inc=(g == 3))
            for g in range(4):
                c0 = 768 + g * 128
                k.op("dve", lambda h, g=g, c0=c0: h.scalar_tensor_tensor(out=ybuf[i].t[:, c0:c0 + 128], in0=bk.t[:, g * 128:(g + 1) * 128], scalar=sgbT.t[:, g:g + 1],
                                                                         in1=ybuf[i].t[:, c0:c0 + 128], op0=ALU.add, op1=ALU.mult),
                     reads=[bk, sgbT, ybuf[i]], writes=[ybuf[i]])
        inproj(2112, 512, sgv)

        def sgu(i, bank):
            hb = hb_r.next()
            k.op("act", lambda h: h.activation(out=hb.t[:, :], in_=bank.t[:, :], func=AF.Gelu_apprx_tanh), reads=[bank], writes=[hb])
            k.op("dve", lambda h: h.tensor_tensor(out=ybuf[i].t[:, 768:1280], in0=ybuf[i].t[:, 768:1280], in1=hb.t[:, :], op=ALU.mult),
                 reads=[hb, ybuf[i]], writes=[ybuf[i]])
        inproj(1600, 512, sgu)

        k.enabled = PH("attn")
        gak3 = gak_T.t.rearrange("(r w) c -> w r c", r=NCORE)
        gav4 = gak_T.t.rearrange("a c -> (a c)").rearrange("(r z) -> r z", r=NCORE)[:, KROWS * 1024:SHROWS * 1024].rearrange(
            "r (i p c) -> p r i c", i=8, p=128)

        class HC:
            pass
        heads = []
        for hh in range(6):
            heads.append(("mla", hh))
            if hh < 4:
                heads.append(("sb", hh))
        chunks, steps = [], []
        for (kind, hh) in heads:
            hc = HC()
            hc.kind, hc.hh = kind, hh
            hc.accs, hc.rs, hc.cry, hc.started = accsets.next(), rs_r.next(), cry_r.next(), set()
            if kind == "mla":
                hc.hrow, hc.hcol, hc.ycol = hh * 128, hh * 128, hh * 128
            else:
                hc.hrow, hc.hcol, hc.ycol = 832 + hh * 128, 768 + hh * 128, 1280 + hh * 128
            for cc in range(16):
                ch = HC()
                ch.hc, ch.cc, ch.loaded = hc, cc, False
                chunks.append(ch)
                for i in range(max(0, 7 - cc // 2), 8):
                    st_ = HC()
                    st_.ch, st_.i, st_.ci = ch, i, len(chunks) - 1
                    st_.first_of_head = (cc == 0 and i == 7)
                    st_.last_of_head = (cc == 15 and i == 7)
                    steps.append(st_)

        def load_chunk(ci):
            ch = chunks[ci]
            if ch.loaded:
                return
            ch.loaded = True
            hc, cc = ch.hc, ch.cc
            ipb, r0 = cc // 2, 4 * (cc % 2)
            ch.kc = kc_r.items[ci % 3]
            ch.vc = vc_r.items[ci % 3]
            load(ch.kc, ch.kc.t[:, :, :], gak3[hc.hrow:hc.hrow + 128, r0:r0 + 4, ipb * 128:(ipb + 1) * 128], src_T=gak_T)
            load(ch.vc, ch.vc.t[:, :, :], gav4[:, r0:r0 + 4, ipb, hc.hcol:hc.hcol + 128], src_T=gav_T)
            if hc.kind == "mla":
                ch.kr = kr_r.items[ci % 3]
                load(ch.kr, ch.kr.t[:, :, :], gak3[768:832, r0:r0 + 4, ipb * 128:(ipb + 1) * 128], src_T=gak_T)

        def S1(sp):
            ch, i = sp.ch, sp.i
            hc, cc = ch.hc, ch.cc
            hh, rs, cry = hc.hh, hc.rs, hc.cry
            if sp.first_of_head:
                if hc.kind == "mla":
                    k.op("dve", lambda h: h.memset(rs.t[:, :, :], 0.0), writes=[rs])
                else:
                    k.op("dve", lambda h: h.memset(cry.t[:, :], 1.0), writes=[cry])
            step = cc - (14 - 2 * i)
            zone = step in (0, 1)
            zc = step * 512
            sbank = mmring.next()
            qs = slice(i * 128, (i + 1) * 128)
            p = p_r.next()
            sp.p = p
            kc_, kflat = ch.kc, ch.kc.t[:, :, :].rearrange("p a b -> p (a b)")
            if hc.kind == "mla":
                kr_ = ch.kr
                k.op("pe", lambda h: h.matmul(sbank.t[:, :], qTn.t[:, hh, qs], kflat, start=True, stop=False),
                     reads=[qTn, kc_], writes=[sbank], inc=False)
                k.op("pe", lambda h: h.matmul(sbank.t[:, :], qTr.t[:, hh, qs], kr_.t[:, :, :].rearrange("p a b -> p (a b)"),
                                              start=False, stop=True), reads=[qTr, kr_], writes=[sbank])
                if not zone:
                    k.op("act", lambda h: h.activation(out=p.t[:, :], in_=sbank.t[:, :], func=AF.Exp, accum_out=rs.t[:, i, step:step + 1]),
                         reads=[sbank], writes=[p, rs])
                else:
                    pf = pf_r.next()
                    k.op("act", lambda h: h.activation(out=pf.t[:, :], in_=sbank.t[:, :], func=AF.Exp), reads=[sbank], writes=[pf])
                    k.op("dve", lambda h: h.scalar_tensor_tensor(out=p.t[:, :], in0=pf.t[:, :], scalar=1.0, in1=mmask.t[:, zc:zc + 512],
                                                                 op0=ALU.mult, op1=ALU.mult, accum_out=rs.t[:, i, step:step + 1]),
                         reads=[pf, mmask], writes=[p, rs])
            else:
                k.op("pe", lambda h: h.matmul(sbank.t[:, :], sbqT.t[:, hh, qs], kflat, start=True, stop=True), reads=[sbqT, kc_], writes=[sbank])
                gb = g_r.next()
                k.op("act", lambda h: h.activation(out=gb.t[:, :], in_=sbank.t[:, :], func=AF.Sigmoid, scale=-1.0), reads=[sbank], writes=[gb])
                if zone:
                    k.op("dve", lambda h: h.tensor_tensor(out=gb.t[:, :], in0=gb.t[:, :], in1=sbm1.t[:, zc:zc + 512], op=ALU.max),
                         reads=[gb, sbm1], writes=[gb])
                eb = e_r.next()
                k.op("dve", lambda h: h.tensor_copy(out=eb.t[:, 0:1], in_=cry.t[:, i:i + 1]), reads=[cry], writes=[eb])
                k.op("dve", lambda h: h.tensor_tensor_scan(out=eb.t[:, 1:513], data0=gb.t[:, :], data1=zeros.t[:, :], initial=cry.t[:, i:i + 1],
                                                           op0=ALU.mult, op1=ALU.add), reads=[gb, zeros, cry], writes=[eb])
                k.op("dve", lambda h: h.tensor_copy(out=cry.t[:, i:i + 1], in_=eb.t[:, 512:513]), reads=[eb], writes=[cry])
                k.op("dve", lambda h: h.tensor_tensor(out=p.t[:, :], in0=eb.t[:, 0:512], in1=eb.t[:, 1:513], op=ALU.subtract), reads=[eb], writes=[p])

        def S2(sp):
            p = sp.p
            sp.pt = pt_r.next()
            transposes(sp.pt, sp.pt.t[:, :, :], p, [p.t[:, b * 128:(b + 1) * 128] for b in range(4)],
                       evac=("dve" if sp.ch.hc.kind == "mla" else "act"))

        def S3(sp):
            ch, i, pt = sp.ch, sp.i, sp.pt
            hc, cc, vc_ = ch.hc, ch.cc, ch.vc
            ab = hc.accs[i // 4]
            for b in range(4):
                st_flag = (i // 4) not in hc.started
                hc.started.add(i // 4)
                k.op("pe", lambda h, b=b, st_flag=st_flag: h.matmul(ab.t[:, (i % 4) * 128:(i % 4 + 1) * 128], pt.t[:, b, :], vc_.t[:, b, :],
                                                                    start=st_flag, stop=(cc == 15 and b == 3), skip_group_check=True),
                     reads=[pt, vc_], writes=[ab], inc=(b == 3))
            if sp.last_of_head:
                ycol, rs = hc.ycol, hc.rs
                if hc.kind == "mla":
                    k.op("dve", lambda h: h.reduce_sum(out=rsum.t[:, :], in_=rs.t[:, :, :], axis=AX.X), reads=[rs], writes=[rsum])
                    k.op("dve", lambda h: h.reciprocal(out=rsum.t[:, :], in_=rsum.t[:, :]), reads=[rsum], writes=[rsum])
                for i2 in range(8):
                    ab2 = hc.accs[i2 // 4]
                    aap = ab2.t[:, (i2 % 4) * 128:(i2 % 4 + 1) * 128]
                    if hc.kind == "mla":
                        k.op("dve", lambda h, i2=i2, aap=aap: h.scalar_tensor_tensor(out=ybuf[i2].t[:, ycol:ycol + 128], in0=aap, scalar=rsum.t[:, i2:i2 + 1],
                                                                                     in1=ybuf[i2].t[:, ycol:ycol + 128], op0=ALU.mult, op1=ALU.mult),
                             reads=[ab2, rsum, ybuf[i2]], writes=[ybuf[i2]])
                    else:
                        k.op("dve", lambda h, i2=i2, aap=aap: h.tensor_tensor(out=ybuf[i2].t[:, ycol:ycol + 128], in0=aap, in1=ybuf[i2].t[:, ycol:ycol + 128],
                                                                              op=ALU.mult), reads=[ab2, ybuf[i2]], writes=[ybuf[i2]])

        NS = len(steps)
        for it in range(NS + 2):
            base = steps[min(max(it - 2, 0), NS - 1)].ci
            for ci in range(base, min(base + 3, len(chunks))):
                load_chunk(ci)
            if it < NS:
                S1(steps[it])
            if 1 <= it <= NS:
                S2(steps[it - 1])
            if it >= 2:
                S3(steps[it - 2])

        k.enabled = True
        if dbg and l == depth - 1:
            for i in range(8):
                store(ydbg_T, ydbg_d[i * 128:(i + 1) * 128, :], ybuf[i], ybuf[i].t[:, :])

        k.enabled = PH("out")
        for i in range(8):
            for g4 in range(4):
                transposes(xT, xT.t[:, g4 * 4:(g4 + 1) * 4, i * 128:(i + 1) * 128], ybuf[i],
                           [ybuf[i].t[:, (g4 * 4 + j) * 128:(g4 * 4 + j + 1) * 128] for j in range(4)], evac=("dve" if g4 % 2 == 0 else "act"))
        for c8 in range(8):
            cs = slice(c8 * 256, (c8 + 1) * 256)
            w = wload(w_out[l, :, cs], 256)
            for i in range(8):
                bank = mmring.next()
                for kc in range(16):
                    k.op("pe", lambda h, kc=kc, i=i, bank=bank, w=w: h.matmul(bank.t[:, 0:256], xT.t[:, kc, i * 128:(i + 1) * 128], w.t[:, kc, 0:256],
                                                                          start=(kc == 0), stop=(kc == 15)), reads=[xT, w], writes=[bank], inc=(kc == 15))
                zp = zp_r.next()
                load(zp, zp.t[:, :], xsrc[i * 128:(i + 1) * 128, cs], src_T=xsrc_T[i])
                k.op("dve", lambda h, i=i, bank=bank, zp=zp, c8=c8: h.scalar_tensor_tensor(out=zp.t[:, :], in0=zp.t[:, :], scalar=ALPHA, in1=bank.t[:, 0:256],
                                                                                         op0=ALU.mult, op1=ALU.add, accum_out=lnsum.t[:, i, c8:c8 + 1]),
                     reads=[bank, zp], writes=[zp, lnsum])
                zq = zq_r.next()
                k.op("act", lambda h, i=i, zp=zp, zq=zq, c8=c8: h.activation(out=zq.t[:, :], in_=zp.t[:, :], func=AF.Square, accum_out=lnsq.t[:, i, c8:c8 + 1]),
                     reads=[zp], writes=[zq, lnsq])
                store(xs_T[i], xs_d[i * 128:(i + 1) * 128, cs], zp, zp.t[:, :])
        k.op("dve", lambda h: h.reduce_sum(out=lnst.t[:, :, 0], in_=lnsum.t[:, :, :], axis=AX.X), reads=[lnsum], writes=[lnst])
        k.op("dve", lambda h: h.reduce_sum(out=lnst.t[:, :, 1], in_=lnsq.t[:, :, :], axis=AX.X), reads=[lnsq], writes=[lnst])
        k.op("dve", lambda h: h.tensor_scalar(out=lnst.t[:, :, 0:2], in0=lnst.t[:, :, 0:2], scalar1=1.0 / D, scalar2=None, op0=ALU.mult), reads=[lnst], writes=[lnst])
        k.op("dve", lambda h: h.tensor_tensor(out=lnst.t[:, :, 2], in0=lnst.t[:, :, 0], in1=lnst.t[:, :, 0], op=ALU.mult), reads=[lnst], writes=[lnst])
        k.op("dve", lambda h: h.tensor_tensor(out=lnst.t[:, :, 1], in0=lnst.t[:, :, 1], in1=lnst.t[:, :, 2], op=ALU.subtract), reads=[lnst], writes=[lnst])
        k.op("act", lambda h: h.activation(out=lnst.t[:, :, 1], in_=lnst.t[:, :, 1], func=AF.Sqrt, bias=eps5.t[:, 0:1], scale=1.0), reads=[lnst, eps5], writes=[lnst])
        k.op("dve", lambda h: h.reciprocal(out=lnst.t[:, :, 1], in_=lnst.t[:, :, 1]), reads=[lnst], writes=[lnst])
        for c8 in range(8):
            cs = slice(c8 * 256, (c8 + 1) * 256)
            lb = lnring.next()
            load(lb, lb.t[:, 0, :], ln_g[l][cs].partition_broadcast(128))
            load(lb, lb.t[:, 1, :], ln_b[l][cs].partition_broadcast(128))
            for i in range(8):
                zp = zp_r.next()
                load(zp, zp.t[:, :], xs_d[i * 128:(i + 1) * 128, cs], src_T=xs_T[i])
                k.op("dve", lambda h, i=i, zp=zp: h.tensor_scalar(out=zp.t[:, :], in0=zp.t[:, :], scalar1=lnst.t[:, i, 0:1], scalar2=lnst.t[:, i, 1:2],
                                                                op0=ALU.subtract, op1=ALU.mult), reads=[zp, lnst], writes=[zp])
                k.op("pool", lambda h, zp=zp, lb=lb: h.tensor_tensor(out=zp.t[:, :], in0=zp.t[:, :], in1=lb.t[:, 0, :], op=ALU.mult), reads=[zp, lb], writes=[zp])
                k.op("pool", lambda h, zp=zp, lb=lb: h.tensor_tensor(out=zp.t[:, :], in0=zp.t[:, :], in1=lb.t[:, 1, :], op=ALU.add), reads=[zp, lb], writes=[zp])
                if l == depth - 1:
                    store(out_T[i], out_d[i * 128:(i + 1) * 128, cs], zp, zp.t[:, :])
                else:
                    store(xs_T[i], xs_d[i * 128:(i + 1) * 128, cs], zp, zp.t[:, :])
    k.enabled = True
    fin = out_T + ([ydbg_T, dg_T, dp_T, dk_T, dq_T, de_T] if dbg else [])
    k.op("sp", lambda h: h.nop(), reads=fin)
    k.emit()
    return nc


def _host_consts(c):
    ident = np.eye(128, dtype=np.float32)
    jrev = ident[::-1].copy()
    zc = np.arange(1024)
    z, pp = zc // 128, zc % 128
    kpos = (7 - z) * 128 + (127 - pp)
    qpos = (7 - c) * 128 + np.arange(128)
    mmask = (kpos[None, :] // 64 <= qpos[:, None] // 64).astype(np.float32)
    sbm1 = (kpos[None, :] >= qpos[:, None]).astype(np.float32)
    inv_freq = (np.float32(10000.0) ** (-np.arange(0, 64, 2, dtype=np.float32) / np.float32(64.0))).astype(np.float32)
    pin = np.arange(128) // 64
    sgmask = (pin[None, :] <= pin[:, None]).astype(np.float32)
    return {
        "ident": ident.astype(ml_dtypes.bfloat16), "jrev": jrev.astype(ml_dtypes.bfloat16),
        "mmask": mmask.astype(ml_dtypes.bfloat16), "sbm1": sbm1,
        "invf": np.broadcast_to(inv_freq[None, :], (128, 32)).copy(), "sgmask": sgmask,
    }


_NC_CACHE = {}


def kernel(x, mem, positions, w_in, q_norm_g, w_uq, kv_norm_g, w_ukv, sg_ln_g, sg_ln_b, sg_w, sg_b,
           w_mem_k, w_mem_v, w_out, ln_g, ln_b, _depth=DEPTH, _dbg=False):
    f = lambda a: np.ascontiguousarray(np.asarray(a, dtype=np.float32))
    x2 = f(x)[0]
    pos = np.asarray(positions)[0].astype(np.int32)
    shared = {"mem": f(mem)[0], "w_in": f(w_in), "q_norm_g": f(q_norm_g), "w_uq": f(w_uq), "kv_norm_g": f(kv_norm_g),
              "w_ukv": f(w_ukv), "sg_ln_g": f(sg_ln_g), "sg_ln_b": f(sg_ln_b), "sg_w": f(sg_w), "sg_b": f(sg_b),
              "w_mem_k": f(w_mem_k), "w_mem_v": f(w_mem_v), "w_out": f(w_out), "ln_g": f(ln_g), "ln_b": f(ln_b)}
    in_maps = []
    for c in range(NCORE):
        blocks = [8 * i + 7 - c for i in range(8)]
        xc = np.concatenate([x2[g * 128:(g + 1) * 128] for g in blocks], axis=0)
        pc = np.stack([pos[g * 128:(g + 1) * 128] for g in blocks], axis=1)
        m = {"x": np.ascontiguousarray(xc), "pos": np.ascontiguousarray(pc)}
        m.update(shared)
        m.update(_host_consts(c))
        in_maps.append(m)
    key = (_depth, _dbg)
    if key not in _NC_CACHE:
        _NC_CACHE[key] = build(_depth, _dbg)
    nc = _NC_CACHE[key]
    res = run_bass_kernel_spmd(nc, in_maps, core_ids=list(range(NCORE)))
    out = np.empty((1, 8192, D), np.float32)
    ydbg = np.empty((8192, D), np.float32) if _dbg else None
    for c in range(NCORE):
        r = res.results[c]
        for i in range(8):
            g = 8 * i + 7 - c
            out[0, g * 128:(g + 1) * 128] = r["out"][i * 128:(i + 1) * 128]
            if _dbg:
                ydbg[g * 128:(g + 1) * 128] = np.asarray(r["ydbg"][i * 128:(i + 1) * 128], dtype=np.float32)
    if _dbg:
        return out, ydbg
    return out
```

```python
import math
import numpy as np
import ml_dtypes
import concourse.bass as bass
import concourse.mybir as mybir
from concourse.bass_utils import run_bass_kernel_spmd

F32, BF16, I32 = mybir.dt.float32, mybir.dt.bfloat16, mybir.dt.int32
AF = mybir.ActivationFunctionType
ALU = mybir.AluOpType
AX = mybir.AxisListType

DEPTH = 4
D = 2048
DIN = 5696
NCORE = 8
KROWS = 6 * 128 + 64 + 4 * 128
VCOLS = 6 * 128 + 4 * 128
ALPHA = (2.0 * DEPTH) ** 0.25
MLA_SCALE = 1.0 / math.sqrt(192.0)
SB_SCALE = 1.0 / math.sqrt(128.0)
MEM_SCALE = 1.0 / math.sqrt(64.0)
TWO_PI = 2.0 * math.pi


class Sem:
    def __init__(self, h, step):
        self.h, self.step, self.n = h, step, 0


class T:
    def __init__(self, t=None, excl=False):
        self.t, self.w, self.r, self.excl = t, None, {}, excl


class Eng:
    def __init__(self, name, sem):
        self.name, self.sem, self.q, self.seen = name, sem, [], {}


class Ring:
    def __init__(self, items):
        self.items, self.i = items, 0

    def next(self):
        it = self.items[self.i % len(self.items)]
        self.i += 1
        return it


class K:
    def __init__(self, nc):
        self.nc = nc
        self.eng = {}
        for n in ("pe", "act", "dve", "pool", "sp"):
            self.eng[n] = Eng(n, Sem(nc.alloc_semaphore(name="s_" + n), 1))
        self.nsem = 0

    def dsem(self, step=16):
        self.nsem += 1
        return Sem(self.nc.alloc_semaphore(name="d%d" % self.nsem), step)

    SEM_LIMIT = 3000

    def op(self, en, fn, reads=(), writes=(), inc=True, sem=None, merge=False):
        if not getattr(self, 'enabled', True):
            return None
        e = self.eng[en]
        deps = []
        for t in reads:
            if t.w is not None:
                deps.extend(t.w.items())
            if t.excl:
                deps.extend(t.r.items())
        for t in writes:
            if t.w is not None:
                deps.extend(t.w.items())
            deps.extend(t.r.items())
        need = {}
        for (s, v) in deps:
            if en == "pe" and s is e.sem and sem is None:
                continue
            if e.seen.get(s, 0) < v:
                need[s] = max(need.get(s, 0), v)
        for s, v in need.items():
            e.q.append(("w", s, v))
            e.seen[s] = v
        if sem is None:
            s = e.sem
            if inc:
                s.n += 1
                ev = (s, s.n)
                e.q.append(("i", fn, s, 1))
                if s.n >= self.SEM_LIMIT:
                    self.nsem += 1
                    e.sem = Sem(self.nc.alloc_semaphore(name="s_%s_%d" % (en, self.nsem)), 1)
            else:
                ev = (s, s.n + 1)
                e.q.append(("i", fn, None, 0))
        else:
            sem.n += sem.step
            ev = (sem, sem.n)
            e.q.append(("i", fn, sem, sem.step))
        for t in writes:
            if merge and t.w is not None:
                t.w = dict(t.w)
                t.w[ev[0]] = max(t.w.get(ev[0], 0), ev[1])
                t.r = {}
            else:
                t.w, t.r = {ev[0]: ev[1]}, {}
        for t in reads:
            if ev[1] > t.r.get(ev[0], 0):
                t.r[ev[0]] = ev[1]
        return ev

    def emit(self):
        def run(e, h):
            for it in e.q:
                if it[0] == "w":
                    h.wait_ge(it[1].h, it[2])
                else:
                    ins = it[1](h)
                    if it[2] is not None:
                        if it[3] == 1:
                            ins.then_inc(it[2].h)
                        else:
                            ins.then_inc(it[2].h, it[3])
        with self.nc.Block() as block:
            @block.tensor
            def _(h):
                run(self.eng["pe"], h)

            @block.scalar
            def _(h):
                run(self.eng["act"], h)

            @block.vector
            def _(h):
                run(self.eng["dve"], h)

            @block.gpsimd
            def _(h):
                run(self.eng["pool"], h)

            @block.sync
            def _(h):
                run(self.eng["sp"], h)


GATE_CHUNKS = ((832, 0, 512), (1344, 512, 256), (2624, 768, 512), (4672, 1280, 512), (5440, 1792, 256))
C1_2PI = 6.28125
C2_2PI = TWO_PI - 6.28125


def build(depth=DEPTH, dbg=False, phases=None):
    PH = lambda name: phases is None or name in phases
    nc = bass.Bass("TRN2", target_bir_lowering=False)
    k = K(nc)

    def din(name, shape, dt=F32):
        return nc.dram_tensor(name, shape, dt, kind="ExternalInput").ap()

    x_d = din("x", [1024, D])
    pos_d = din("pos", [128, 8], I32)
    mem_d = din("mem", [256, D])
    w_in = din("w_in", [DEPTH, D, DIN])
    q_norm_g = din("q_norm_g", [DEPTH, 512])
    w_uq = din("w_uq", [DEPTH, 512, 1152])
    kv_norm_g = din("kv_norm_g", [DEPTH, 256])
    w_ukv = din("w_ukv", [DEPTH, 256, 1536])
    sg_ln_g = din("sg_ln_g", [DEPTH, 512])
    sg_ln_b = din("sg_ln_b", [DEPTH, 512])
    sg_w = din("sg_w", [DEPTH, 4, 128, 128])
    sg_b = din("sg_b", [DEPTH, 4, 128])
    w_mem_k = din("w_mem_k", [DEPTH, D, 256])
    w_mem_v = din("w_mem_v", [DEPTH, D, 256])
    w_out = din("w_out", [DEPTH, D, D])
    ln_g = din("ln_g", [DEPTH, D])
    ln_b = din("ln_b", [DEPTH, D])
    ident_d = din("ident", [128, 128], BF16)
    jrev_d = din("jrev", [128, 128], BF16)
    mmask_d = din("mmask", [128, 1024], BF16)
    sbm1_d = din("sbm1", [128, 1024], F32)
    invf_d = din("invf", [128, 32], F32)
    sgmask_d = din("sgmask", [128, 128], F32)
    out_d = nc.dram_tensor("out", [1024, D], F32, kind="ExternalOutput").ap()
    out_T = [T(out_d) for _ in range(8)]
    if dbg:
        ydbg_d = nc.dram_tensor("ydbg", [1024, D], BF16, kind="ExternalOutput").ap()
        ydbg_T = T(ydbg_d)
        dg_T = T(nc.dram_tensor("d_g", [2, 128, 512], F32, kind="ExternalOutput").ap())
        dp_T = T(nc.dram_tensor("d_p", [2, 128, 512], BF16, kind="ExternalOutput").ap())
        dk_T = T(nc.dram_tensor("d_k", [2, 128, 512], BF16, kind="ExternalOutput").ap())
        dq_T = T(nc.dram_tensor("d_q", [128, 128], BF16, kind="ExternalOutput").ap())
        de_T = T(nc.dram_tensor("d_e", [2, 128, 513], F32, kind="ExternalOutput").ap())

    xs_d = nc.dram_tensor("xs", [1024, D], F32, kind="Internal").ap()
    xs_T = [T(xs_d) for _ in range(8)]
    SHROWS = KROWS + VCOLS
    shk_T = T(nc.dram_tensor("sh", [SHROWS, 1024], BF16, kind="Internal").ap())
    shv_T = shk_T
    gak_T = T(nc.dram_tensor("ga", [NCORE * SHROWS, 1024], BF16, kind="Internal", addr_space="Shared").ap())
    gav_T = gak_T
    shv_ap = shk_T.t[KROWS:SHROWS, :].rearrange("r c -> (r c)").rearrange("(p v) -> p v", v=VCOLS)
    ccsem = k.dsem(1)

    def sb(name, shape, dt):
        return T(nc.alloc_sbuf_tensor("sb_" + name, shape, dt))

    def sbr(name, shape, dt, n=2):
        return Ring([sb("%s%d" % (name, j), shape, dt) for j in range(n)])

    banks = [T(nc.alloc_psum_tensor("ps%d" % j, [128, 512], F32), excl=True) for j in range(8)]
    mmring = Ring(banks[0:2])
    trring = Ring(banks[2:4])
    accsets = Ring([(banks[4], banks[5]), (banks[6], banks[7])])
    smring = Ring(banks[4:8])

    def tsem(t):
        if not hasattr(t, "sem"):
            t.sem = k.dsem()
        return t.sem

    def load(dst, dst_ap, src_ap, q="sp", src_T=None, **kw):
        k.op(q, lambda h: h.dma_start(out=dst_ap, in_=src_ap, **kw), reads=[src_T] if src_T is not None else [],
             writes=[dst], sem=tsem(dst))

    def store(dst_T, dst_ap, src, src_ap, q="sp"):
        if not hasattr(src, "ssem"):
            src.ssem = k.dsem()
        k.op(q, lambda h: h.dma_start(out=dst_ap, in_=src_ap), reads=[src], writes=[dst_T], sem=src.ssem, merge=True)

    ident = sb("ident", [128, 128], BF16)
    jrev = sb("jrev", [128, 128], BF16)
    mmask = sb("mmask", [128, 1024], BF16)
    sbm1 = sb("sbm1", [128, 1024], F32)
    zeros = sb("zeros", [128, 512], F32)
    sgmask = sb("sgmask", [128, 128], F32)
    cosT = sb("cosT", [128, 8, 32], F32)
    sinT = sb("sinT", [128, 8, 32], F32)
    eps6 = sb("eps6", [128, 1], F32)
    eps5 = sb("eps5", [128, 1], F32)
    xT = sb("xT", [128, 16, 1024], BF16)
    ybuf = [sb("ybuf%d" % i, [128, D], BF16) for i in range(8)]
    qTn = sb("qTn", [128, 6, 1024], BF16)
    qTr = sb("qTr", [64, 6, 1024], BF16)
    sbqT = sb("sbqT", [128, 4, 1024], BF16)
    wring = sbr("wr", [128, 16, 256], BF16, 3)
    wsmall = sb("wsmall", [128, 4608], BF16)
    mkT = sb("mkT", [64, 4, 256], BF16)
    mv = sb("mv", [128, 2, 256], BF16)
    wspT = sb("wspT", [128, 4, 128], BF16)
    sgbT = sb("sgbT", [128, 4], F32)
    qg_bc = sb("qg_bc", [128, 512], F32)
    kvg_bc = sb("kvg_bc", [128, 256], F32)
    sgg_bc = sb("sgg_bc", [128, 512], F32)
    sgb_bc = sb("sgb_bc", [128, 512], F32)
    lnring = sbr("lnbc", [128, 2, 256], F32, 2)
    f32_r = sbr("f32r", [128, 512], F32, 3)
    bf_r = sbr("bfr", [128, 512], BF16, 4)
    xin_r = hseg_r = g_r = f32_r
    xb_r = hb_r = p_r = pf_r = bf_r
    st_r = sbr("st", [128, 8], F32, 4)
    cqnT = sb("cqnT", [128, 4, 128], BF16)
    kvnT = sb("kvnT", [128, 2, 128], BF16)
    qfr = sb("qfr", [128, 6, 64], F32)
    qb = sb("qb", [128, 1152], BF16)
    rt_r = sbr("rt", [128, 4, 32], F32, 2)
    kvb = sb("kvb", [128, 1536], BF16)
    kpeb = sb("kpeb", [128, 64], BF16)
    kst_r = sbr("kst", [128, 4, 128], BF16, 2)
    kpst = sb("kpst", [64, 128], BF16)
    vst_r = sbr("vst", [128, 768], BF16, 2)
    mqT = sb("mqT", [64, 4, 128], BF16)
    pm_r = sbr("pm", [128, 256], BF16, 2)
    pmT_r = sbr("pmT", [128, 2, 128], BF16, 2)
    kc_r = sbr("kc", [128, 4, 128], BF16, 3)
    kr_r = sbr("kr", [64, 4, 128], BF16, 3)
    vc_r = sbr("vc", [128, 4, 128], BF16, 3)
    e_r = sbr("eb", [128, 513], F32, 2)
    pt_r = sbr("ptb", [128, 4, 128], BF16, 2)
    rs_r = sbr("rs", [128, 8, 16], F32, 2)
    cry_r = sbr("cry", [128, 8], F32, 2)
    rsum = sb("rsum", [128, 8], F32)
    lnsum = sb("lnsum", [128, 8, 8], F32)
    lnsq = sb("lnsq", [128, 8, 8], F32)
    lnst = sb("lnst", [128, 8, 4], F32)
    zp_r = sbr("zp", [128, 256], F32, 3)
    zq_r = sbr("zq", [128, 256], F32, 1)

    load(ident, ident.t[:, :], ident_d[:, :])
    load(jrev, jrev.t[:, :], jrev_d[:, :])
    load(mmask, mmask.t[:, :], mmask_d[:, :])
    load(sbm1, sbm1.t[:, :], sbm1_d[:, :])
    load(sgmask, sgmask.t[:, :], sgmask_d[:, :])
    k.op("dve", lambda h: h.memset(zeros.t[:, :], 0.0), writes=[zeros])
    k.op("dve", lambda h: h.memset(eps6.t[:, :], 1e-6), writes=[eps6])
    k.op("dve", lambda h: h.memset(eps5.t[:, :], 1e-5), writes=[eps5])

    posi = sb("posi", [128, 8], I32)
    posf = sb("posf", [128, 8], F32)
    invf = sb("invf", [128, 32], F32)
    class _View:
        pass

    def view3(tile):
        v = T(tile.t[:, 0:256].rearrange("p (a b) -> p a b", a=8))
        return v
    _a, _b, _c = f32_r.next(), f32_r.next(), f32_r.next()
    ang, tq, tkf = view3(_a), view3(_b), view3(_c)
    tki = sb("tki", [128, 8, 32], I32)
    load(invf, invf.t[:, :], invf_d[:, :])
    load(posi, posi.t[:, :], pos_d[:, :])
    k.op("dve", lambda h: h.tensor_copy(out=posf.t[:, :], in_=posi.t[:, :]), reads=[posi], writes=[posf])
    for i in range(8):
        k.op("dve", lambda h, i=i: h.tensor_scalar(out=ang.t[:, i, :], in0=invf.t[:, :], scalar1=posf.t[:, i:i + 1],
                                                   scalar2=None, op0=ALU.mult), reads=[invf, posf], writes=[ang])
    k.op("dve", lambda h: h.tensor_scalar(out=tki.t[:, :, :], in0=ang.t[:, :, :], scalar1=1.0 / TWO_PI, scalar2=None, op0=ALU.mult),
         reads=[ang], writes=[tki])
    k.op("dve", lambda h: h.tensor_copy(out=tkf.t[:, :, :], in_=tki.t[:, :, :]), reads=[tki], writes=[tkf])
    k.op("dve", lambda h: h.scalar_tensor_tensor(out=ang.t[:, :, :], in0=tkf.t[:, :, :], scalar=-C1_2PI, in1=ang.t[:, :, :],
                                                 op0=ALU.mult, op1=ALU.add), reads=[tkf, ang], writes=[ang])
    k.op("dve", lambda h: h.scalar_tensor_tensor(out=ang.t[:, :, :], in0=tkf.t[:, :, :], scalar=-C2_2PI, in1=ang.t[:, :, :],
                                                 op0=ALU.mult, op1=ALU.add), reads=[tkf, ang], writes=[ang])
    k.op("dve", lambda h: h.tensor_scalar(out=tq.t[:, :, :], in0=ang.t[:, :, :], scalar1=3.14159, scalar2=-3.14159, op0=ALU.min, op1=ALU.max),
         reads=[ang], writes=[tq])
    k.op("act", lambda h: h.activation(out=sinT.t[:, :, :], in_=tq.t[:, :, :], func=AF.Sin), reads=[tq], writes=[sinT])
    k.op("dve", lambda h: h.tensor_scalar(out=ang.t[:, :, :], in0=ang.t[:, :, :], scalar1=0.5 * math.pi, scalar2=None, op0=ALU.add),
         reads=[ang], writes=[ang])
    k.op("dve", lambda h: h.tensor_scalar(out=tkf.t[:, :, :], in0=ang.t[:, :, :], scalar1=math.pi, scalar2=-TWO_PI, op0=ALU.is_gt, op1=ALU.mult),
         reads=[ang], writes=[tkf])
    k.op("dve", lambda h: h.tensor_tensor(out=ang.t[:, :, :], in0=ang.t[:, :, :], in1=tkf.t[:, :, :], op=ALU.add), reads=[ang, tkf], writes=[ang])
    k.op("dve", lambda h: h.tensor_scalar(out=tq.t[:, :, :], in0=ang.t[:, :, :], scalar1=3.14159, scalar2=-3.14159, op0=ALU.min, op1=ALU.max),
         reads=[ang], writes=[tq])
    k.op("act", lambda h: h.activation(out=cosT.t[:, :, :], in_=tq.t[:, :, :], func=AF.Sin), reads=[tq], writes=[cosT])
    for (ring_t, v) in ((_a, ang), (_b, tq), (_c, tkf)):
        ring_t.w, ring_t.r = (dict(v.w) if v.w else None), dict(v.r)

    def transposes(dst, dst_ap, src, src_aps, rev=False, evac="dve", m=128):
        bank = trring.next()
        n = len(src_aps)
        idm = jrev if rev else ident
        for j, sap in enumerate(src_aps):
            k.op("pe", lambda h, j=j, sap=sap: h.matmul(bank.t[0:m, j * 128:(j + 1) * 128], sap, idm.t[:, :], start=True, stop=True),
                 reads=[src, idm], writes=[bank], inc=(j == n - 1))
        src_ap = bank.t[0:m, 0:n * 128]
        if len(dst_ap.shape) == 3:
            src_ap = src_ap.rearrange("p (a b) -> p a b", a=n)
        if evac == "dve":
            k.op("dve", lambda h: h.tensor_copy(out=dst_ap, in_=src_ap), reads=[bank], writes=[dst])
        else:
            k.op("act", lambda h: h.copy(out=dst_ap, in_=src_ap), reads=[bank], writes=[dst])

    def wload(l_ap, width):
        w = wring.next()
        for q4 in range(4):
            load(w, w.t[:, q4 * 4:(q4 + 1) * 4, 0:width], l_ap[q4 * 512:(q4 + 1) * 512, :].rearrange("(kc p) e -> p kc e", p=128), q="pool")
        return w

    def rope(src, s0, s1, i, dst, d0, d1, eng="dve"):
        rt = rt_r.next()
        c, s = cosT.t[:, i, :], sinT.t[:, i, :]
        k.op(eng, lambda h: h.tensor_tensor(out=rt.t[:, 0, :], in0=s0, in1=c, op=ALU.mult), reads=[src, cosT], writes=[rt])
        k.op(eng, lambda h: h.tensor_tensor(out=rt.t[:, 1, :], in0=s1, in1=s, op=ALU.mult), reads=[src, sinT], writes=[rt])
        k.op(eng, lambda h: h.tensor_tensor(out=rt.t[:, 2, :], in0=s0, in1=s, op=ALU.mult), reads=[src, sinT], writes=[rt])
        k.op(eng, lambda h: h.tensor_tensor(out=rt.t[:, 3, :], in0=s1, in1=c, op=ALU.mult), reads=[src, cosT], writes=[rt])
        k.op(eng, lambda h: h.tensor_tensor(out=d0, in0=rt.t[:, 0, :], in1=rt.t[:, 1, :], op=ALU.subtract), reads=[rt], writes=[dst])
        k.op(eng, lambda h: h.tensor_tensor(out=d1, in0=rt.t[:, 2, :], in1=rt.t[:, 3, :], op=ALU.add), reads=[rt], writes=[dst])

    def rstd(st, col, n, eps_t):
        k.op("act", lambda h: h.activation(out=st.t[:, col:col + 1], in_=st.t[:, col:col + 1], func=AF.Sqrt, bias=eps_t.t[:, 0:1], scale=1.0 / n),
             reads=[st, eps_t], writes=[st])
        k.op("dve", lambda h: h.reciprocal(out=st.t[:, col:col + 1], in_=st.t[:, col:col + 1]), reads=[st], writes=[st])

    def bcv(dst, dap, vec):
        load(dst, dap, vec.partition_broadcast(128))

    def params(l):
        k.enabled = PH("params")
        bcv(qg_bc, qg_bc.t[:, :], q_norm_g[l])
        bcv(kvg_bc, kvg_bc.t[:, :], kv_norm_g[l])
        bcv(sgg_bc, sgg_bc.t[:, :], sg_ln_g[l])
        bcv(sgb_bc, sgb_bc.t[:, :], sg_ln_b[l])
        load(sgbT, sgbT.t[:, :], sg_b[l].rearrange("g t -> t g"), allow_slow_non_contiguous=True)
        for g in range(4):
            xin = xin_r.next()
            load(xin, xin.t[:, 0:128], sg_w[l, g])
            k.op("dve", lambda h, xin=xin: h.tensor_tensor(out=xin.t[:, 0:128], in0=xin.t[:, 0:128], in1=sgmask.t[:, :], op=ALU.mult),
                 reads=[xin, sgmask], writes=[xin])
            xb = xb_r.next()
            k.op("dve", lambda h, xb=xb, xin=xin: h.tensor_copy(out=xb.t[:, 0:128], in_=xin.t[:, 0:128]), reads=[xin], writes=[xb])
            transposes(wspT, wspT.t[:, g, :], xb, [xb.t[:, 0:128]])
        memT = wring.next()
        for mb in range(2):
            for pc in range(4):
                xin = xin_r.next()
                load(xin, xin.t[:, :], mem_d[mb * 128:(mb + 1) * 128, pc * 512:(pc + 1) * 512])
                xb = xb_r.next()
                k.op("dve", lambda h, xb=xb, xin=xin: h.tensor_copy(out=xb.t[:, :], in_=xin.t[:, :]), reads=[xin], writes=[xb])
                transposes(memT, memT.t[:, pc * 4:(pc + 1) * 4, mb * 128:(mb + 1) * 128], xb, [xb.t[:, j * 128:(j + 1) * 128] for j in range(4)])
        w = wload(w_mem_k[l], 256)
        for hh in range(4):
            bank = smring.next()
            for kc in range(16):
                k.op("pe", lambda h, kc=kc, hh=hh, bank=bank, w=w, memT=memT: h.matmul(bank.t[0:64, 0:256], w.t[:, kc, hh * 64:(hh + 1) * 64], memT.t[:, kc, :],
                                                                      start=(kc == 0), stop=(kc == 15)), reads=[w, memT], writes=[bank], inc=(kc == 15))
            k.op("dve", lambda h, hh=hh, bank=bank: h.tensor_copy(out=mkT.t[:, hh, :], in_=bank.t[0:64, 0:256]), reads=[bank], writes=[mkT])
        w = wload(w_mem_v[l], 256)
        for mb in range(2):
            bank = smring.next()
            for kc in range(16):
                k.op("pe", lambda h, kc=kc, mb=mb, bank=bank, w=w, memT=memT: h.matmul(bank.t[:, 0:256], memT.t[:, kc, mb * 128:(mb + 1) * 128], w.t[:, kc, 0:256],
                                                                      start=(kc == 0), stop=(kc == 15)), reads=[w, memT], writes=[bank], inc=(kc == 15))
            k.op("dve", lambda h, mb=mb, bank=bank: h.tensor_copy(out=mv.t[:, mb, :], in_=bank.t[:, 0:256]), reads=[bank], writes=[mv])


    for l in range(depth):
        xsrc = x_d if l == 0 else xs_d
        xsrc_T = [None] * 8 if l == 0 else xs_T

        if l == 0:
            params(0)
        k.enabled = PH("xT")
        for i in range(8):
            for pc in range(4):
                xin = xin_r.next()
                load(xin, xin.t[:, :], xsrc[i * 128:(i + 1) * 128, pc * 512:(pc + 1) * 512], src_T=xsrc_T[i])
                xb = xb_r.next()
                k.op("act", lambda h, xb=xb, xin=xin: h.copy(out=xb.t[:, :], in_=xin.t[:, :]), reads=[xin], writes=[xb])
                transposes(xT, xT.t[:, pc * 4:(pc + 1) * 4, i * 128:(i + 1) * 128], xb, [xb.t[:, j * 128:(j + 1) * 128] for j in range(4)],
                           evac=("dve" if pc % 2 == 0 else "act"))

        def inproj(c0, width, handler):
            pieces = [(p0, min(256, width - p0)) for p0 in range(0, width, 256)]
            ws = [wload(w_in[l, :, c0 + p0:c0 + p0 + pw], pw) for (p0, pw) in pieces]
            pend = None
            for i in range(8):
                bank = mmring.next()
                for pi, (p0, pw) in enumerate(pieces):
                    for kc in range(16):
                        k.op("pe", lambda h, kc=kc, i=i, bank=bank, w=ws[pi], p0=p0, pw=pw: h.matmul(
                            bank.t[:, p0:p0 + pw], xT.t[:, kc, i * 128:(i + 1) * 128], w.t[:, kc, 0:pw], start=(kc == 0), stop=(kc == 15)),
                            reads=[xT, ws[pi]], writes=[bank], inc=(kc == 15 and pi == len(pieces) - 1))
                if pend is not None:
                    handler(*pend)
                pend = (i, bank)
            handler(*pend)

        k.enabled = PH("gates")
        for (c0, ycol, width) in GATE_CHUNKS:
            def gate(i, bank, ycol=ycol, width=width):
                k.op("act", lambda h: h.activation(out=ybuf[i].t[:, ycol:ycol + width], in_=bank.t[:, 0:width], func=AF.Silu),
                     reads=[bank], writes=[ybuf[i]])
            inproj(c0, width, gate)

        k.enabled = PH("kv")
        load(wsmall, wsmall.t[:, 0:3072].rearrange("p (a b) -> p a b", a=2), w_ukv[l].rearrange("(kc p) e -> p kc e", p=128), q="pool")

        def kv_handler(i, bank):
            ip = 7 - i
            hs = hseg_r.next()
            st = st_r.next()
            hb = hb_r.next()
            k.op("act", lambda h: h.copy(out=hs.t[:, 0:320], in_=bank.t[:, 0:320]), reads=[bank], writes=[hs])
            k.op("act", lambda h: h.activation(out=hb.t[:, 0:256], in_=hs.t[:, 0:256], func=AF.Square, accum_out=st.t[:, 0:1]),
                 reads=[hs], writes=[hb, st])
            rstd(st, 0, 256.0, eps6)
            k.op("dve", lambda h: h.scalar_tensor_tensor(out=hb.t[:, 0:256], in0=hs.t[:, 0:256], scalar=st.t[:, 0:1], in1=kvg_bc.t[:, :],
                                                         op0=ALU.mult, op1=ALU.mult), reads=[hs, st, kvg_bc], writes=[hb])
            transposes(kvnT, kvnT.t[:, :, :], hb, [hb.t[:, 0:128], hb.t[:, 128:256]])
            for n3 in range(3):
                bk = smring.next()
                for kc in range(2):
                    k.op("pe", lambda h, kc=kc, n3=n3, bk=bk: h.matmul(bk.t[:, :], kvnT.t[:, kc, :], wsmall.t[:, kc * 1536 + n3 * 512:kc * 1536 + (n3 + 1) * 512],
                                                                     start=(kc == 0), stop=(kc == 1)), reads=[kvnT, wsmall], writes=[bk], inc=(kc == 1))
                k.op("act", lambda h, n3=n3, bk=bk: h.copy(out=kvb.t[:, n3 * 512:(n3 + 1) * 512], in_=bk.t[:, :]), reads=[bk], writes=[kvb])
            for hp in range(2):
                ks = kst_r.next()
                nh = 4 if hp == 0 else 2
                transposes(ks, ks.t[:, 0:nh, :], kvb, [kvb.t[:, (hp * 4 + j) * 256:(hp * 4 + j) * 256 + 128] for j in range(nh)], rev=True)
                store(shk_T, shk_T.t[hp * 512:hp * 512 + nh * 128, ip * 128:(ip + 1) * 128].rearrange("(a p) c -> p a c", p=128), ks, ks.t[:, 0:nh, :])
            vs = vst_r.next()
            kv3 = kvb.t[:, :].rearrange("p (a b) -> p a b", b=256)
            for hp in range(2):
                bk = smring.next()
                k.op("pe", lambda h, hp=hp, bk=bk: h.matmul(bk.t[:, 0:384].rearrange("p (a b) -> p a b", a=3), jrev.t[:, :],
                                                            kv3[:, hp * 3:(hp + 1) * 3, 128:256], start=True, stop=True), reads=[kvb, jrev], writes=[bk])
                k.op("dve", lambda h, hp=hp, bk=bk: h.tensor_copy(out=vs.t[:, hp * 384:(hp + 1) * 384], in_=bk.t[:, 0:384]), reads=[bk], writes=[vs])
            store(shv_T, shv_ap[ip * 128:(ip + 1) * 128, 0:768], vs, vs.t[:, :])
            rope(hs, hs.t[:, 256:288], hs.t[:, 288:320], i, kpeb, kpeb.t[:, 0:32], kpeb.t[:, 32:64])
            transposes(kpst, kpst.t[:, :], kpeb, [kpeb.t[:, :]], rev=True, m=64)
            store(shk_T, shk_T.t[768:832, ip * 128:(ip + 1) * 128], kpst, kpst.t[:, :])
        inproj(512, 320, kv_handler)

        def sbk(i, bank):
            ip = 7 - i
            hb = hb_r.next()
            k.op("act", lambda h: h.copy(out=hb.t[:, :], in_=bank.t[:, :]), reads=[bank], writes=[hb])
            ks = kst_r.next()
            transposes(ks, ks.t[:, :, :], hb, [hb.t[:, j * 128:(j + 1) * 128] for j in range(4)], rev=True)
            store(shk_T, shk_T.t[832:1344, ip * 128:(ip + 1) * 128].rearrange("(a p) c -> p a c", p=128), ks, ks.t[:, :, :])
        inproj(3648, 512, sbk)

        def sbv(i, bank):
            ip = 7 - i
            hb = hb_r.next()
            k.op("act", lambda h: h.copy(out=hb.t[:, :], in_=bank.t[:, :]), reads=[bank], writes=[hb])
            bk = smring.next()
            k.op("pe", lambda h: h.matmul(bk.t[:, :], jrev.t[:, :], hb.t[:, :], start=True, stop=True), reads=[hb, jrev], writes=[bk])
            vs = vst_r.next()
            k.op("dve", lambda h: h.tensor_copy(out=vs.t[:, 0:512], in_=bk.t[:, :]), reads=[bk], writes=[vs])
            store(shv_T, shv_ap[ip * 128:(ip + 1) * 128, 768:1280], vs, vs.t[:, 0:512])
        inproj(4160, 512, sbv)

        k.enabled = PH("exch")
        k.op("pool", lambda h: h.collective_compute("AllGather", ALU.bypass, replica_groups=[list(range(NCORE))],
                                                    ins=[shk_T.t[:, :]], outs=[gak_T.t[:, :]]), reads=[shk_T], writes=[gak_T], sem=ccsem)

        k.enabled = PH("q")
        load(wsmall, wsmall.t[:, 0:4608].rearrange("p (a b) -> p a b", a=4), w_uq[l].rearrange("(kc p) e -> p kc e", p=128), q="pool")

        def cq_handler(i, bank):
            hs = hseg_r.next()
            st = st_r.next()
            hb = hb_r.next()
            k.op("act", lambda h: h.copy(out=hs.t[:, :], in_=bank.t[:, :]), reads=[bank], writes=[hs])
            k.op("act", lambda h: h.activation(out=hb.t[:, :], in_=hs.t[:, :], func=AF.Square, accum_out=st.t[:, 0:1]), reads=[hs], writes=[hb, st])
            rstd(st, 0, 512.0, eps6)
            k.op("dve", lambda h: h.scalar_tensor_tensor(out=hb.t[:, :], in0=hs.t[:, :], scalar=st.t[:, 0:1], in1=qg_bc.t[:, :],
                                                         op0=ALU.mult, op1=ALU.mult), reads=[hs, st, qg_bc], writes=[hb])
            transposes(cqnT, cqnT.t[:, :, :], hb, [hb.t[:, j * 128:(j + 1) * 128] for j in range(4)])
            for n3 in range(3):
                bk = smring.next()
                for kc in range(4):
                    k.op("pe", lambda h, kc=kc, n3=n3, bk=bk: h.matmul(bk.t[:, 0:384], cqnT.t[:, kc, :], wsmall.t[:, kc * 1152 + n3 * 384:kc * 1152 + (n3 + 1) * 384],
                                                                     start=(kc == 0), stop=(kc == 3)), reads=[cqnT, wsmall], writes=[bk], inc=(kc == 3))
                k.op("act", lambda h, n3=n3, bk=bk: h.activation(out=qb.t[:, n3 * 384:(n3 + 1) * 384], in_=bk.t[:, 0:384], func=AF.Copy, scale=MLA_SCALE),
                     reads=[bk], writes=[qb])
                k.op("dve", lambda h, n3=n3, bk=bk: h.tensor_scalar(out=qfr.t[:, 2 * n3:2 * n3 + 2, :],
                                                                   in0=bk.t[:, 0:384].rearrange("p (a b) -> p a b", a=2)[:, :, 128:192],
                                                                   scalar1=MLA_SCALE, scalar2=None, op0=ALU.mult), reads=[bk], writes=[qfr])
            for hh in range(6):
                b0 = hh * 192 + 128
                rope(qfr, qfr.t[:, hh, 0:32], qfr.t[:, hh, 32:64], i, qb, qb.t[:, b0:b0 + 32], qb.t[:, b0 + 32:b0 + 64],
                     eng="dve")
            for g4 in (0, 4):
                nh = 4 if g4 == 0 else 2
                transposes(qTn, qTn.t[:, g4:g4 + nh, i * 128:(i + 1) * 128], qb, [qb.t[:, (g4 + j) * 192:(g4 + j) * 192 + 128] for j in range(nh)])
                transposes(qTr, qTr.t[:, g4:g4 + nh, i * 128:(i + 1) * 128], qb, [qb.t[:, (g4 + j) * 192 + 128:(g4 + j) * 192 + 192] for j in range(nh)],
                           m=64, evac="act")
        inproj(0, 512, cq_handler)

        k.enabled = PH("sbq")
        def sbq(i, bank):
            hb = hb_r.next()
            k.op("act", lambda h: h.activation(out=hb.t[:, :], in_=bank.t[:, :], func=AF.Copy, scale=SB_SCALE), reads=[bank], writes=[hb])
            transposes(sbqT, sbqT.t[:, :, i * 128:(i + 1) * 128], hb, [hb.t[:, j * 128:(j + 1) * 128] for j in range(4)])
        inproj(3136, 512, sbq)

        k.enabled = PH("mem")
        def mqh(i, bank):
            hb = hb_r.next()
            st = st_r.next()
            k.op("act", lambda h: h.activation(out=hb.t[:, 0:256], in_=bank.t[:, 0:256], func=AF.Copy, scale=MEM_SCALE), reads=[bank], writes=[hb])
            transposes(mqT, mqT.t[:, :, :], hb, [hb.t[:, j * 64:(j + 1) * 64] for j in range(4)], m=64)
            obank = smring.next()
            sbanks = (smring.next(), smring.next())
            for hh in range(4):
                sbk_ = sbanks[hh // 2]
                sc0 = (hh % 2) * 256
                k.op("pe", lambda h, hh=hh, sbk_=sbk_, sc0=sc0: h.matmul(sbk_.t[:, sc0:sc0 + 256], mqT.t[:, hh, :], mkT.t[:, hh, :], start=True, stop=True),
                     reads=[mqT, mkT], writes=[sbk_])
                pm = pm_r.next()
                k.op("act", lambda h, hh=hh, sbk_=sbk_, pm=pm, sc0=sc0: h.activation(out=pm.t[:, :], in_=sbk_.t[:, sc0:sc0 + 256], func=AF.Exp,
                                                                                     accum_out=st.t[:, hh:hh + 1]),
                     reads=[sbk_], writes=[pm, st])
                pmT = pmT_r.next()
                transposes(pmT, pmT.t[:, :, :], pm, [pm.t[:, 0:128], pm.t[:, 128:256]])
                for mb in range(2):
                    k.op("pe", lambda h, hh=hh, mb=mb, pmT=pmT: h.matmul(obank.t[:, hh * 64:(hh + 1) * 64], pmT.t[:, mb, :], mv.t[:, mb, hh * 64:(hh + 1) * 64],
                                                                         start=(mb == 0), stop=(mb == 1)), reads=[pmT, mv], writes=[obank], inc=(mb == 1))
            k.op("dve", lambda h: h.reciprocal(out=st.t[:, 4:8], in_=st.t[:, 0:4]), reads=[st], writes=[st])
            for hh in range(4):
                c0 = 1792 + hh * 64
                k.op("dve", lambda h, hh=hh, c0=c0: h.scalar_tensor_tensor(out=ybuf[i].t[:, c0:c0 + 64], in0=obank.t[:, hh * 64:(hh + 1) * 64],
                                                                           scalar=st.t[:, 4 + hh:5 + hh], in1=ybuf[i].t[:, c0:c0 + 64],
                                                                           op0=ALU.mult, op1=ALU.mult), reads=[obank, st, ybuf[i]], writes=[ybuf[i]])
        inproj(5184, 256, mqh)

        k.enabled = PH("gmlp")
        def sgv(i, bank):
            hs = hseg_r.next()
            st = st_r.next()
            hb = hb_r.next()
            k.op("act", lambda h: h.activation(out=hs.t[:, :], in_=bank.t[:, :], func=AF.Gelu_apprx_tanh, accum_out=st.t[:, 0:1]), reads=[bank], writes=[hs, st])
            k.op("dve", lambda h: h.tensor_scalar(out=st.t[:, 0:1], in0=st.t[:, 0:1], scalar1=1.0 / 512.0, scalar2=None, op0=ALU.mult), reads=[st], writes=[st])
            k.op("dve", lambda h: h.tensor_scalar(out=hs.t[:, :], in0=hs.t[:, :], scalar1=st.t[:, 0:1], scalar2=None, op0=ALU.subtract), reads=[hs, st], writes=[hs])
            k.op("act", lambda h: h.activation(out=hb.t[:, :], in_=hs.t[:, :], func=AF.Square, accum_out=st.t[:, 1:2]), reads=[hs], writes=[hb, st])
            rstd(st, 1, 512.0, eps5)
            k.op("dve", lambda h: h.scalar_tensor_tensor(out=hs.t[:, :], in0=hs.t[:, :], scalar=st.t[:, 1:2], in1=sgg_bc.t[:, :], op0=ALU.mult, op1=ALU.mult),
                 reads=[hs, st, sgg_bc], writes=[hs])
            k.op("dve", lambda h: h.tensor_tensor(out=hb.t[:, :], in0=hs.t[:, :], in1=sgb_bc.t[:, :], op=ALU.add), reads=[hs, sgb_bc], writes=[hb])
            bk = smring.next()
            for g in range(4):
                k.op("pe", lambda h, g=g: h.matmul(bk.t[:, g * 128:(g + 1) * 128], wspT.t[:, g, :], hb.t[:, g * 128:(g + 1) * 128], start=True, stop=True),
                     reads=[wspT, hb], writes=[bk], inc=(g == 3))
            for g in range(4):
                c0 = 768 + g * 128
                k.op("dve", lambda h, g=g, c0=c0: h.scalar_tensor_tensor(out=ybuf[i].t[:, c0:c0 + 128], in0=bk.t[:, g * 128:(g + 1) * 128], scalar=sgbT.t[:, g:g + 1],
                                                                         in1=ybuf[i].t[:, c0:c0 + 128], op0=ALU.add, op1=ALU.mult),
                     reads=[bk, sgbT, ybuf[i]], writes=[ybuf[i]])
        inproj(2112, 512, sgv)

        def sgu(i, bank):
            hb = hb_r.next()
            k.op("act", lambda h: h.activation(out=hb.t[:, :], in_=bank.t[:, :], func=AF.Gelu_apprx_tanh), reads=[bank], writes=[hb])
            k.op("dve", lambda h: h.tensor_tensor(out=ybuf[i].t[:, 768:1280], in0=ybuf[i].t[:, 768:1280], in1=hb.t[:, :], op=ALU.mult),
                 reads=[hb, ybuf[i]], writes=[ybuf[i]])
        inproj(1600, 512, sgu)

        k.enabled = PH("attn")
        gak3 = gak_T.t.rearrange("(r w) c -> w r c", r=NCORE)
        gav4 = gak_T.t.rearrange("a c -> (a c)").rearrange("(r z) -> r z", r=NCORE)[:, KROWS * 1024:SHROWS * 1024].rearrange(
            "r (i p c) -> p r i c", i=8, p=128)

        class HC:
            pass
        heads = []
        for hh in range(6):
            heads.append(("mla", hh))
            if hh < 4:
                heads.append(("sb", hh))
        chunks, steps = [], []
        for (kind, hh) in heads:
            hc = HC()
            hc.kind, hc.hh = kind, hh
            hc.accs, hc.rs, hc.cry, hc.started = accsets.next(), rs_r.next(), cry_r.next(), set()
            if kind == "mla":
                hc.hrow, hc.hcol, hc.ycol = hh * 128, hh * 128, hh * 128
            else:
                hc.hrow, hc.hcol, hc.ycol = 832 + hh * 128, 768 + hh * 128, 1280 + hh * 128
            for cc in range(16):
                ch = HC()
                ch.hc, ch.cc, ch.loaded = hc, cc, False
                chunks.append(ch)
                for i in range(max(0, 7 - cc // 2), 8):
                    st_ = HC()
                    st_.ch, st_.i, st_.ci = ch, i, len(chunks) - 1
                    st_.first_of_head = (cc == 0 and i == 7)
                    st_.last_of_head = (cc == 15 and i == 7)
                    steps.append(st_)

        def load_chunk(ci):
            ch = chunks[ci]
            if ch.loaded:
                return
            ch.loaded = True
            hc, cc = ch.hc, ch.cc
            ipb, r0 = cc // 2, 4 * (cc % 2)
            ch.kc = kc_r.items[ci % 3]
            ch.vc = vc_r.items[ci % 3]
            load(ch.kc, ch.kc.t[:, :, :], gak3[hc.hrow:hc.hrow + 128, r0:r0 + 4, ipb * 128:(ipb + 1) * 128], src_T=gak_T)
            load(ch.vc, ch.vc.t[:, :, :], gav4[:, r0:r0 + 4, ipb, hc.hcol:hc.hcol + 128], src_T=gav_T)
            if hc.kind == "mla":
                ch.kr = kr_r.items[ci % 3]
                load(ch.kr, ch.kr.t[:, :, :], gak3[768:832, r0:r0 + 4, ipb * 128:(ipb + 1) * 128], src_T=gak_T)

        def S1(sp):
            ch, i = sp.ch, sp.i
            hc, cc = ch.hc, ch.cc
            hh, rs, cry = hc.hh, hc.rs, hc.cry
            if sp.first_of_head:
                if hc.kind == "mla":
                    k.op("dve", lambda h: h.memset(rs.t[:, :, :], 0.0), writes=[rs])
                else:
                    k.op("dve", lambda h: h.memset(cry.t[:, :], 1.0), writes=[cry])
            step = cc - (14 - 2 * i)
            zone = step in (0, 1)
            zc = step * 512
            sbank = mmring.next()
            qs = slice(i * 128, (i + 1) * 128)
            p = p_r.next()
            sp.p = p
            kc_, kflat = ch.kc, ch.kc.t[:, :, :].rearrange("p a b -> p (a b)")
            if hc.kind == "mla":
                kr_ = ch.kr
                k.op("pe", lambda h: h.matmul(sbank.t[:, :], qTn.t[:, hh, qs], kflat, start=True, stop=False),
                     reads=[qTn, kc_], writes=[sbank], inc=False)
                k.op("pe", lambda h: h.matmul(sbank.t[:, :], qTr.t[:, hh, qs], kr_.t[:, :, :].rearrange("p a b -> p (a b)"),
                                              start=False, stop=True), reads=[qTr, kr_], writes=[sbank])
                if not zone:
                    k.op("act", lambda h: h.activation(out=p.t[:, :], in_=sbank.t[:, :], func=AF.Exp, accum_out=rs.t[:, i, step:step + 1]),
                         reads=[sbank], writes=[p, rs])
                else:
                    pf = pf_r.next()
                    k.op("act", lambda h: h.activation(out=pf.t[:, :], in_=sbank.t[:, :], func=AF.Exp), reads=[sbank], writes=[pf])
                    k.op("dve", lambda h: h.scalar_tensor_tensor(out=p.t[:, :], in0=pf.t[:, :], scalar=1.0, in1=mmask.t[:, zc:zc + 512],
                                                                 op0=ALU.mult, op1=ALU.mult, accum_out=rs.t[:, i, step:step + 1]),
                         reads=[pf, mmask], writes=[p, rs])
            else:
                k.op("pe", lambda h: h.matmul(sbank.t[:, :], sbqT.t[:, hh, qs], kflat, start=True, stop=True), reads=[sbqT, kc_], writes=[sbank])
                gb = g_r.next()
                k.op("act", lambda h: h.activation(out=gb.t[:, :], in_=sbank.t[:, :], func=AF.Sigmoid, scale=-1.0), reads=[sbank], writes=[gb])
                if zone:
                    k.op("dve", lambda h: h.tensor_tensor(out=gb.t[:, :], in0=gb.t[:, :], in1=sbm1.t[:, zc:zc + 512], op=ALU.max),
                         reads=[gb, sbm1], writes=[gb])
                eb = e_r.next()
                k.op("dve", lambda h: h.tensor_copy(out=eb.t[:, 0:1], in_=cry.t[:, i:i + 1]), reads=[cry], writes=[eb])
                k.op("dve", lambda h: h.tensor_tensor_scan(out=eb.t[:, 1:513], data0=gb.t[:, :], data1=zeros.t[:, :], initial=cry.t[:, i:i + 1],
                                                           op0=ALU.mult, op1=ALU.add), reads=[gb, zeros, cry], writes=[eb])
                k.op("dve", lambda h: h.tensor_copy(out=cry.t[:, i:i + 1], in_=eb.t[:, 512:513]), reads=[eb], writes=[cry])
                k.op("dve", lambda h: h.tensor_tensor(out=p.t[:, :], in0=eb.t[:, 0:512], in1=eb.t[:, 1:513], op=ALU.subtract), reads=[eb], writes=[p])

        def S2(sp):
            p = sp.p
            sp.pt = pt_r.next()
            transposes(sp.pt, sp.pt.t[:, :, :], p, [p.t[:, b * 128:(b + 1) * 128] for b in range(4)],
                       evac=("dve" if sp.ch.hc.kind == "mla" else "act"))

        def S3(sp):
            ch, i, pt = sp.ch, sp.i, sp.pt
            hc, cc, vc_ = ch.hc, ch.cc, ch.vc
            ab = hc.accs[i // 4]
            for b in range(4):
                st_flag = (i // 4) not in hc.started
                hc.started.add(i // 4)
                k.op("pe", lambda h, b=b, st_flag=st_flag: h.matmul(ab.t[:, (i % 4) * 128:(i % 4 + 1) * 128], pt.t[:, b, :], vc_.t[:, b, :],
                                                                    start=st_flag, stop=(cc == 15 and b == 3), skip_group_check=True),
                     reads=[pt, vc_], writes=[ab], inc=(b == 3))
            if sp.last_of_head:
                ycol, rs = hc.ycol, hc.rs
                if hc.kind == "mla":
                    k.op("dve", lambda h: h.reduce_sum(out=rsum.t[:, :], in_=rs.t[:, :, :], axis=AX.X), reads=[rs], writes=[rsum])
                    k.op("dve", lambda h: h.reciprocal(out=rsum.t[:, :], in_=rsum.t[:, :]), reads=[rsum], writes=[rsum])
                for i2 in range(8):
                    ab2 = hc.accs[i2 // 4]
                    aap = ab2.t[:, (i2 % 4) * 128:(i2 % 4 + 1) * 128]
                    if hc.kind == "mla":
                        k.op("dve", lambda h, i2=i2, aap=aap: h.scalar_tensor_tensor(out=ybuf[i2].t[:, ycol:ycol + 128], in0=aap, scalar=rsum.t[:, i2:i2 + 1],
                                                                                     in1=ybuf[i2].t[:, ycol:ycol + 128], op0=ALU.mult, op1=ALU.mult),
                             reads=[ab2, rsum, ybuf[i2]], writes=[ybuf[i2]])
                    else:
                        k.op("dve", lambda h, i2=i2, aap=aap: h.tensor_tensor(out=ybuf[i2].t[:, ycol:ycol + 128], in0=aap, in1=ybuf[i2].t[:, ycol:ycol + 128],
                                                                              op=ALU.mult), reads=[ab2, ybuf[i2]], writes=[ybuf[i2]])

        NS = len(steps)
        for it in range(NS + 2):
            base = steps[min(max(it - 2, 0), NS - 1)].ci
            for ci in range(base, min(base + 3, len(chunks))):
                load_chunk(ci)
            if it < NS:
                S1(steps[it])
            if 1 <= it <= NS:
                S2(steps[it - 1])
            if it >= 2:
                S3(steps[it - 2])

        k.enabled = True
        if dbg and l == depth - 1:
            for i in range(8):
                store(ydbg_T, ydbg_d[i * 128:(i + 1) * 128, :], ybuf[i], ybuf[i].t[:, :])

        if l + 1 < depth:
            params(l + 1)
        k.enabled = PH("out")
        for i in range(8):
            for g4 in range(4):
                transposes(xT, xT.t[:, g4 * 4:(g4 + 1) * 4, i * 128:(i + 1) * 128], ybuf[i],
                           [ybuf[i].t[:, (g4 * 4 + j) * 128:(g4 * 4 + j + 1) * 128] for j in range(4)], evac=("dve" if g4 % 2 == 0 else "act"))
        for c8 in range(8):
            cs = slice(c8 * 256, (c8 + 1) * 256)
            w = wload(w_out[l, :, cs], 256)
            for i in range(8):
                bank = mmring.next()
                for kc in range(16):
                    k.op("pe", lambda h, kc=kc, i=i, bank=bank, w=w: h.matmul(bank.t[:, 0:256], xT.t[:, kc, i * 128:(i + 1) * 128], w.t[:, kc, 0:256],
                                                                          start=(kc == 0), stop=(kc == 15)), reads=[xT, w], writes=[bank], inc=(kc == 15))
                zp = zp_r.next()
                load(zp, zp.t[:, :], xsrc[i * 128:(i + 1) * 128, cs], src_T=xsrc_T[i])
                k.op("dve", lambda h, i=i, bank=bank, zp=zp, c8=c8: h.scalar_tensor_tensor(out=zp.t[:, :], in0=zp.t[:, :], scalar=ALPHA, in1=bank.t[:, 0:256],
                                                                                         op0=ALU.mult, op1=ALU.add, accum_out=lnsum.t[:, i, c8:c8 + 1]),
                     reads=[bank, zp], writes=[zp, lnsum])
                zq = zq_r.next()
                k.op("act", lambda h, i=i, zp=zp, zq=zq, c8=c8: h.activation(out=zq.t[:, :], in_=zp.t[:, :], func=AF.Square, accum_out=lnsq.t[:, i, c8:c8 + 1]),
                     reads=[zp], writes=[zq, lnsq])
                store(xs_T[i], xs_d[i * 128:(i + 1) * 128, cs], zp, zp.t[:, :])
        k.op("dve", lambda h: h.reduce_sum(out=lnst.t[:, :, 0], in_=lnsum.t[:, :, :], axis=AX.X), reads=[lnsum], writes=[lnst])
        k.op("dve", lambda h: h.reduce_sum(out=lnst.t[:, :, 1], in_=lnsq.t[:, :, :], axis=AX.X), reads=[lnsq], writes=[lnst])
        k.op("dve", lambda h: h.tensor_scalar(out=lnst.t[:, :, 0:2], in0=lnst.t[:, :, 0:2], scalar1=1.0 / D, scalar2=None, op0=ALU.mult), reads=[lnst], writes=[lnst])
        k.op("dve", lambda h: h.tensor_tensor(out=lnst.t[:, :, 2], in0=lnst.t[:, :, 0], in1=lnst.t[:, :, 0], op=ALU.mult), reads=[lnst], writes=[lnst])
        k.op("dve", lambda h: h.tensor_tensor(out=lnst.t[:, :, 1], in0=lnst.t[:, :, 1], in1=lnst.t[:, :, 2], op=ALU.subtract), reads=[lnst], writes=[lnst])
        k.op("act", lambda h: h.activation(out=lnst.t[:, :, 1], in_=lnst.t[:, :, 1], func=AF.Sqrt, bias=eps5.t[:, 0:1], scale=1.0), reads=[lnst, eps5], writes=[lnst])
        k.op("dve", lambda h: h.reciprocal(out=lnst.t[:, :, 1], in_=lnst.t[:, :, 1]), reads=[lnst], writes=[lnst])
        for c8 in range(8):
            cs = slice(c8 * 256, (c8 + 1) * 256)
            lb = lnring.next()
            load(lb, lb.t[:, 0, :], ln_g[l][cs].partition_broadcast(128))
            load(lb, lb.t[:, 1, :], ln_b[l][cs].partition_broadcast(128))
            for i in range(8):
                zp = zp_r.next()
                load(zp, zp.t[:, :], xs_d[i * 128:(i + 1) * 128, cs], src_T=xs_T[i])
                k.op("dve", lambda h, i=i, zp=zp: h.tensor_scalar(out=zp.t[:, :], in0=zp.t[:, :], scalar1=lnst.t[:, i, 0:1], scalar2=lnst.t[:, i, 1:2],
                                                                op0=ALU.subtract, op1=ALU.mult), reads=[zp, lnst], writes=[zp])
                k.op("pool", lambda h, zp=zp, lb=lb: h.tensor_tensor(out=zp.t[:, :], in0=zp.t[:, :], in1=lb.t[:, 0, :], op=ALU.mult), reads=[zp, lb], writes=[zp])
                k.op("pool", lambda h, zp=zp, lb=lb: h.tensor_tensor(out=zp.t[:, :], in0=zp.t[:, :], in1=lb.t[:, 1, :], op=ALU.add), reads=[zp, lb], writes=[zp])
                if l == depth - 1:
                    store(out_T[i], out_d[i * 128:(i + 1) * 128, cs], zp, zp.t[:, :])
                else:
                    store(xs_T[i], xs_d[i * 128:(i + 1) * 128, cs], zp, zp.t[:, :])
    k.enabled = True
    fin = out_T + ([ydbg_T, dg_T, dp_T, dk_T, dq_T, de_T] if dbg else [])
    k.op("sp", lambda h: h.nop(), reads=fin)
    k.emit()
    return nc


def _host_consts(c):
    ident = np.eye(128, dtype=np.float32)
    jrev = ident[::-1].copy()
    zc = np.arange(1024)
    z, pp = zc // 128, zc % 128
    kpos = (7 - z) * 128 + (127 - pp)
    qpos = (7 - c) * 128 + np.arange(128)
    mmask = (kpos[None, :] // 64 <= qpos[:, None] // 64).astype(np.float32)
    sbm1 = (kpos[None, :] >= qpos[:, None]).astype(np.float32)
    inv_freq = (np.float32(10000.0) ** (-np.arange(0, 64, 2, dtype=np.float32) / np.float32(64.0))).astype(np.float32)
    pin = np.arange(128) // 64
    sgmask = (pin[None, :] <= pin[:, None]).astype(np.float32)
    return {
        "ident": ident.astype(ml_dtypes.bfloat16), "jrev": jrev.astype(ml_dtypes.bfloat16),
        "mmask": mmask.astype(ml_dtypes.bfloat16), "sbm1": sbm1,
        "invf": np.broadcast_to(inv_freq[None, :], (128, 32)).copy(), "sgmask": sgmask,
    }


_NC_CACHE = {}


def kernel(x, mem, positions, w_in, q_norm_g, w_uq, kv_norm_g, w_ukv, sg_ln_g, sg_ln_b, sg_w, sg_b,
           w_mem_k, w_mem_v, w_out, ln_g, ln_b, _depth=DEPTH, _dbg=False):
    f = lambda a: np.ascontiguousarray(np.asarray(a, dtype=np.float32))
    x2 = f(x)[0]
    pos = np.asarray(positions)[0].astype(np.int32)
    shared = {"mem": f(mem)[0], "w_in": f(w_in), "q_norm_g": f(q_norm_g), "w_uq": f(w_uq), "kv_norm_g": f(kv_norm_g),
              "w_ukv": f(w_ukv), "sg_ln_g": f(sg_ln_g), "sg_ln_b": f(sg_ln_b), "sg_w": f(sg_w), "sg_b": f(sg_b),
              "w_mem_k": f(w_mem_k), "w_mem_v": f(w_mem_v), "w_out": f(w_out), "ln_g": f(ln_g), "ln_b": f(ln_b)}
    in_maps = []
    for c in range(NCORE):
        blocks = [8 * i + 7 - c for i in range(8)]
        xc = np.concatenate([x2[g * 128:(g + 1) * 128] for g in blocks], axis=0)
        pc = np.stack([pos[g * 128:(g + 1) * 128] for g in blocks], axis=1)
        m = {"x": np.ascontiguousarray(xc), "pos": np.ascontiguousarray(pc)}
        m.update(shared)
        m.update(_host_consts(c))
        in_maps.append(m)
    key = (_depth, _dbg)
    if key not in _NC_CACHE:
        _NC_CACHE[key] = build(_depth, _dbg)
    nc = _NC_CACHE[key]
    res = run_bass_kernel_spmd(nc, in_maps, core_ids=list(range(NCORE)))
    out = np.empty((1, 8192, D), np.float32)
    ydbg = np.empty((8192, D), np.float32) if _dbg else None
    for c in range(NCORE):
        r = res.results[c]
        for i in range(8):
            g = 8 * i + 7 - c
            out[0, g * 128:(g + 1) * 128] = r["out"][i * 128:(i + 1) * 128]
            if _dbg:
                ydbg[g * 128:(g + 1) * 128] = np.asarray(r["ydbg"][i * 128:(i + 1) * 128], dtype=np.float32)
    if _dbg:
        return out, ydbg
    return out
```

```python
import math
import numpy as np
import ml_dtypes
import concourse.bass as bass
import concourse.mybir as mybir
from concourse.bass_utils import run_bass_kernel_spmd

F32, BF16, I32 = mybir.dt.float32, mybir.dt.bfloat16, mybir.dt.int32
AF = mybir.ActivationFunctionType
ALU = mybir.AluOpType
AX = mybir.AxisListType

DEPTH = 4
D = 2048
DIN = 5696
NCORE = 8
KROWS = 6 * 128 + 64 + 4 * 128
VCOLS = 6 * 128 + 4 * 128
ALPHA = (2.0 * DEPTH) ** 0.25
MLA_SCALE = 1.0 / math.sqrt(192.0)
SB_SCALE = 1.0 / math.sqrt(128.0)
MEM_SCALE = 1.0 / math.sqrt(64.0)
TWO_PI = 2.0 * math.pi


class Sem:
    def __init__(self, h, step):
        self.h, self.step, self.n = h, step, 0


class T:
    def __init__(self, t=None, excl=False, share=None):
        self.t, self.excl = t, excl
        self._s = share._s if share is not None else [None, {}]

    @property
    def w(self):
        return self._s[0]

    @w.setter
    def w(self, v):
        self._s[0] = v

    @property
    def r(self):
        return self._s[1]

    @r.setter
    def r(self, v):
        self._s[1] = v


class Eng:
    def __init__(self, name, sem):
        self.name, self.sem, self.q, self.seen = name, sem, [], {}


class Ring:
    def __init__(self, items):
        self.items, self.i = items, 0

    def next(self):
        it = self.items[self.i % len(self.items)]
        self.i += 1
        return it


class K:
    def __init__(self, nc):
        self.nc = nc
        self.eng = {}
        for n in ("pe", "act", "dve", "pool", "sp"):
            self.eng[n] = Eng(n, Sem(nc.alloc_semaphore(name="s_" + n), 1))
        self.nsem = 0

    def dsem(self, step=16):
        self.nsem += 1
        return Sem(self.nc.alloc_semaphore(name="d%d" % self.nsem), step)

    SEM_LIMIT = 3000

    def op(self, en, fn, reads=(), writes=(), inc=True, sem=None, merge=False):
        if not getattr(self, 'enabled', True):
            return None
        e = self.eng[en]
        deps = []
        for t in reads:
            if t.w is not None:
                deps.extend(t.w.items())
            if t.excl:
                deps.extend(t.r.items())
        for t in writes:
            if t.w is not None:
                deps.extend(t.w.items())
            deps.extend(t.r.items())
        need = {}
        for (s, v) in deps:
            if en == "pe" and s is e.sem and sem is None:
                continue
            if e.seen.get(s, 0) < v:
                need[s] = max(need.get(s, 0), v)
        for s, v in need.items():
            e.q.append(("w", s, v))
            e.seen[s] = v
        if sem is None:
            s = e.sem
            if inc:
                s.n += 1
                ev = (s, s.n)
                e.q.append(("i", fn, s, 1))
                if s.n >= self.SEM_LIMIT:
                    self.nsem += 1
                    e.sem = Sem(self.nc.alloc_semaphore(name="s_%s_%d" % (en, self.nsem)), 1)
            else:
                ev = (s, s.n + 1)
                e.q.append(("i", fn, None, 0))
        else:
            sem.n += sem.step
            ev = (sem, sem.n)
            e.q.append(("i", fn, sem, sem.step))
        for t in writes:
            if merge and t.w is not None:
                t.w = dict(t.w)
                t.w[ev[0]] = max(t.w.get(ev[0], 0), ev[1])
                t.r = {}
            else:
                t.w, t.r = {ev[0]: ev[1]}, {}
        for t in reads:
            if ev[1] > t.r.get(ev[0], 0):
                t.r[ev[0]] = ev[1]
        return ev

    def emit(self):
        def run(e, h):
            for it in e.q:
                if it[0] == "w":
                    h.wait_ge(it[1].h, it[2])
                else:
                    ins = it[1](h)
                    if it[2] is not None:
                        if it[3] == 1:
                            ins.then_inc(it[2].h)
                        else:
                            ins.then_inc(it[2].h, it[3])
        with self.nc.Block() as block:
            @block.tensor
            def _(h):
                run(self.eng["pe"], h)

            @block.scalar
            def _(h):
                run(self.eng["act"], h)

            @block.vector
            def _(h):
                run(self.eng["dve"], h)

            @block.gpsimd
            def _(h):
                run(self.eng["pool"], h)

            @block.sync
            def _(h):
                run(self.eng["sp"], h)


GATE_CHUNKS = ((832, 0, 512), (1344, 512, 256), (2624, 768, 512), (4672, 1280, 512), (5440, 1792, 256))
C1_2PI = 6.28125
C2_2PI = TWO_PI - 6.28125


def build(depth=DEPTH, dbg=False, phases=None):
    PH = lambda name: phases is None or name in phases
    nc = bass.Bass("TRN2", target_bir_lowering=False)
    k = K(nc)

    def din(name, shape, dt=F32):
        return nc.dram_tensor(name, shape, dt, kind="ExternalInput").ap()

    x_d = din("x", [1024, D])
    pos_d = din("pos", [128, 8], I32)
    mem_d = din("mem", [256, D])
    w_in = din("w_in", [DEPTH, D, DIN])
    q_norm_g = din("q_norm_g", [DEPTH, 512])
    w_uq = din("w_uq", [DEPTH, 512, 1152])
    kv_norm_g = din("kv_norm_g", [DEPTH, 256])
    w_ukv = din("w_ukv", [DEPTH, 256, 1536])
    sg_ln_g = din("sg_ln_g", [DEPTH, 512])
    sg_ln_b = din("sg_ln_b", [DEPTH, 512])
    sg_w = din("sg_w", [DEPTH, 4, 128, 128])
    sg_b = din("sg_b", [DEPTH, 4, 128])
    w_mem_k = din("w_mem_k", [DEPTH, D, 256])
    w_mem_v = din("w_mem_v", [DEPTH, D, 256])
    w_out = din("w_out", [DEPTH, D, D])
    ln_g = din("ln_g", [DEPTH, D])
    ln_b = din("ln_b", [DEPTH, D])
    ident_d = din("ident", [128, 128], BF16)
    jrev_d = din("jrev", [128, 128], BF16)
    mmask_d = din("mmask", [128, 1024], BF16)
    sbm1_d = din("sbm1", [128, 1024], F32)
    invf_d = din("invf", [128, 32], F32)
    sgmask_d = din("sgmask", [128, 128], F32)
    out_d = nc.dram_tensor("out", [1024, D], F32, kind="ExternalOutput").ap()
    out_T = [T(out_d) for _ in range(8)]
    if dbg:
        ydbg_d = nc.dram_tensor("ydbg", [1024, D], BF16, kind="ExternalOutput").ap()
        ydbg_T = T(ydbg_d)
        dg_T = T(nc.dram_tensor("d_g", [2, 128, 512], F32, kind="ExternalOutput").ap())
        dp_T = T(nc.dram_tensor("d_p", [2, 128, 512], BF16, kind="ExternalOutput").ap())
        dk_T = T(nc.dram_tensor("d_k", [2, 128, 512], BF16, kind="ExternalOutput").ap())
        dq_T = T(nc.dram_tensor("d_q", [128, 128], BF16, kind="ExternalOutput").ap())
        de_T = T(nc.dram_tensor("d_e", [2, 128, 513], F32, kind="ExternalOutput").ap())

    xs_d = nc.dram_tensor("xs", [1024, D], F32, kind="Internal").ap()
    xs_T = [T(xs_d) for _ in range(8)]
    SHROWS = KROWS + VCOLS
    shk_T = T(nc.dram_tensor("sh", [SHROWS, 1024], BF16, kind="Internal").ap())
    shv_T = shk_T
    gak_T = T(nc.dram_tensor("ga", [NCORE * SHROWS, 1024], BF16, kind="Internal", addr_space="Shared").ap())
    gav_T = gak_T
    shv_ap = shk_T.t[KROWS:SHROWS, :].rearrange("r c -> (r c)").rearrange("(p v) -> p v", v=VCOLS)
    ccsem = k.dsem(1)

    def sb(name, shape, dt):
        return T(nc.alloc_sbuf_tensor("sb_" + name, shape, dt))

    def sbr(name, shape, dt, n=2):
        return Ring([sb("%s%d" % (name, j), shape, dt) for j in range(n)])

    banks = [T(nc.alloc_psum_tensor("ps%d" % j, [128, 512], F32), excl=True) for j in range(8)]
    mmring = Ring(banks[0:2])
    trring = Ring(banks[2:4])
    accsets = Ring([(banks[4], banks[5]), (banks[6], banks[7])])
    smring = Ring(banks[4:8])

    def tsem(t):
        if not hasattr(t, "sem"):
            t.sem = k.dsem()
        return t.sem

    def load(dst, dst_ap, src_ap, q="sp", src_T=None, **kw):
        k.op(q, lambda h: h.dma_start(out=dst_ap, in_=src_ap, **kw), reads=[src_T] if src_T is not None else [],
             writes=[dst], sem=tsem(dst))

    def store(dst_T, dst_ap, src, src_ap, q="sp"):
        if not hasattr(src, "ssem"):
            src.ssem = k.dsem()
        k.op(q, lambda h: h.dma_start(out=dst_ap, in_=src_ap), reads=[src], writes=[dst_T], sem=src.ssem, merge=True)

    ident = sb("ident", [128, 128], BF16)
    jrev = sb("jrev", [128, 128], BF16)
    mmask = sb("mmask", [128, 1024], BF16)
    sbm1 = sb("sbm1", [128, 1024], F32)
    zeros = sb("zeros", [128, 512], F32)
    sgmask = sb("sgmask", [128, 128], F32)
    cosT = sb("cosT", [128, 8, 32], F32)
    sinT = sb("sinT", [128, 8, 32], F32)
    eps6 = sb("eps6", [128, 1], F32)
    eps5 = sb("eps5", [128, 1], F32)
    xT = sb("xT", [128, 16, 1024], BF16)
    ybuf = [sb("ybuf%d" % i, [128, D], BF16) for i in range(8)]
    qTn = sb("qTn", [128, 6, 1024], BF16)
    qTr = sb("qTr", [64, 6, 1024], BF16)
    sbqT = sb("sbqT", [128, 4, 1024], BF16)
    wring = sbr("wr", [128, 16, 512], BF16, 2)
    wsmall = sb("wsmall", [128, 4608], BF16)
    mkT = sb("mkT", [64, 4, 256], BF16)
    mv = sb("mv", [128, 2, 256], BF16)
    wspT = sb("wspT", [128, 4, 128], BF16)
    sgbT = sb("sgbT", [128, 4], F32)
    qg_bc = sb("qg_bc", [128, 512], F32)
    kvg_bc = sb("kvg_bc", [128, 256], F32)
    sgg_bc = sb("sgg_bc", [128, 512], F32)
    sgb_bc = sb("sgb_bc", [128, 512], F32)
    f32_r = sbr("f32r", [128, 512], F32, 3)
    bf_r = sbr("bfr", [128, 512], BF16, 4)
    xin_r = hseg_r = g_r = f32_r
    xb_r = hb_r = p_r = pf_r = bf_r
    st_r = sbr("st", [128, 8], F32, 4)
    cqnT = sb("cqnT", [128, 4, 128], BF16)
    kvnT = sb("kvnT", [128, 2, 128], BF16)
    qfr = sb("qfr", [128, 6, 64], F32)
    qb = sb("qb", [128, 1152], BF16)
    rt_r = sbr("rt", [128, 4, 32], F32, 2)
    kvb = sb("kvb", [128, 1536], BF16)
    kpeb = sb("kpeb", [128, 64], BF16)
    kst_r = sbr("kst", [128, 4, 128], BF16, 2)
    kpst = sb("kpst", [64, 128], BF16)
    vst_r = sbr("vst", [128, 768], BF16, 2)
    mqT = sb("mqT", [64, 4, 128], BF16)
    pm_r = sbr("pm", [128, 256], BF16, 2)
    pmT_r = sbr("pmT", [128, 2, 128], BF16, 2)
    kc_r = sbr("kc", [128, 4, 128], BF16, 3)
    kr_r = sbr("kr", [64, 4, 128], BF16, 3)
    vc_r = sbr("vc", [128, 4, 128], BF16, 3)
    e_r = sbr("eb", [128, 513], F32, 2)
    pt_r = sbr("ptb", [128, 4, 128], BF16, 2)
    rs_r = sbr("rs", [128, 8, 16], F32, 2)
    cry_r = sbr("cry", [128, 8], F32, 2)
    rsum = sb("rsum", [128, 8], F32)
    lnsum = sb("lnsum", [128, 8, 8], F32)
    lnsq = sb("lnsq", [128, 8, 8], F32)
    lnst = sb("lnst", [128, 8, 4], F32)
    zp_r = sbr("zp", [128, 256], F32, 3)

    load(ident, ident.t[:, :], ident_d[:, :])
    load(jrev, jrev.t[:, :], jrev_d[:, :])
    load(mmask, mmask.t[:, :], mmask_d[:, :])
    load(sbm1, sbm1.t[:, :], sbm1_d[:, :])
    load(sgmask, sgmask.t[:, :], sgmask_d[:, :])
    k.op("dve", lambda h: h.memset(zeros.t[:, :], 0.0), writes=[zeros])
    k.op("dve", lambda h: h.memset(eps6.t[:, :], 1e-6), writes=[eps6])
    k.op("dve", lambda h: h.memset(eps5.t[:, :], 1e-5), writes=[eps5])

    posi = sb("posi", [128, 8], I32)
    posf = sb("posf", [128, 8], F32)
    invf = sb("invf", [128, 32], F32)
    class _View:
        pass

    def view3(tile):
        return T(tile.t[:, 0:256].rearrange("p (a b) -> p a b", a=8), share=tile)
    _a, _b, _c = f32_r.next(), f32_r.next(), f32_r.next()
    ang, tq, tkf = view3(_a), view3(_b), view3(_c)
    tki = sb("tki", [128, 8, 32], I32)
    load(invf, invf.t[:, :], invf_d[:, :])
    load(posi, posi.t[:, :], pos_d[:, :])
    k.op("dve", lambda h: h.tensor_copy(out=posf.t[:, :], in_=posi.t[:, :]), reads=[posi], writes=[posf])
    for i in range(8):
        k.op("dve", lambda h, i=i: h.tensor_scalar(out=ang.t[:, i, :], in0=invf.t[:, :], scalar1=posf.t[:, i:i + 1],
                                                   scalar2=None, op0=ALU.mult), reads=[invf, posf], writes=[ang])
    k.op("dve", lambda h: h.tensor_scalar(out=tki.t[:, :, :], in0=ang.t[:, :, :], scalar1=1.0 / TWO_PI, scalar2=None, op0=ALU.mult),
         reads=[ang], writes=[tki])
    k.op("dve", lambda h: h.tensor_copy(out=tkf.t[:, :, :], in_=tki.t[:, :, :]), reads=[tki], writes=[tkf])
    k.op("dve", lambda h: h.scalar_tensor_tensor(out=ang.t[:, :, :], in0=tkf.t[:, :, :], scalar=-C1_2PI, in1=ang.t[:, :, :],
                                                 op0=ALU.mult, op1=ALU.add), reads=[tkf, ang], writes=[ang])
    k.op("dve", lambda h: h.scalar_tensor_tensor(out=ang.t[:, :, :], in0=tkf.t[:, :, :], scalar=-C2_2PI, in1=ang.t[:, :, :],
                                                 op0=ALU.mult, op1=ALU.add), reads=[tkf, ang], writes=[ang])
    k.op("dve", lambda h: h.tensor_scalar(out=tq.t[:, :, :], in0=ang.t[:, :, :], scalar1=3.14159, scalar2=-3.14159, op0=ALU.min, op1=ALU.max),
         reads=[ang], writes=[tq])
    k.op("act", lambda h: h.activation(out=sinT.t[:, :, :], in_=tq.t[:, :, :], func=AF.Sin), reads=[tq], writes=[sinT])
    k.op("dve", lambda h: h.tensor_scalar(out=ang.t[:, :, :], in0=ang.t[:, :, :], scalar1=0.5 * math.pi, scalar2=None, op0=ALU.add),
         reads=[ang], writes=[ang])
    k.op("dve", lambda h: h.tensor_scalar(out=tkf.t[:, :, :], in0=ang.t[:, :, :], scalar1=math.pi, scalar2=-TWO_PI, op0=ALU.is_gt, op1=ALU.mult),
         reads=[ang], writes=[tkf])
    k.op("dve", lambda h: h.tensor_tensor(out=ang.t[:, :, :], in0=ang.t[:, :, :], in1=tkf.t[:, :, :], op=ALU.add), reads=[ang, tkf], writes=[ang])
    k.op("dve", lambda h: h.tensor_scalar(out=tq.t[:, :, :], in0=ang.t[:, :, :], scalar1=3.14159, scalar2=-3.14159, op0=ALU.min, op1=ALU.max),
         reads=[ang], writes=[tq])
    k.op("act", lambda h: h.activation(out=cosT.t[:, :, :], in_=tq.t[:, :, :], func=AF.Sin), reads=[tq], writes=[cosT])

    def transposes(dst, dst_ap, src, src_aps, rev=False, evac="dve", m=128):
        bank = trring.next()
        n = len(src_aps)
        idm = jrev if rev else ident
        for j, sap in enumerate(src_aps):
            k.op("pe", lambda h, j=j, sap=sap: h.matmul(bank.t[0:m, j * 128:(j + 1) * 128], sap, idm.t[:, :], start=True, stop=True),
                 reads=[src, idm], writes=[bank], inc=(j == n - 1))
        src_ap = bank.t[0:m, 0:n * 128]
        if len(dst_ap.shape) == 3:
            src_ap = src_ap.rearrange("p (a b) -> p a b", a=n)
        if evac == "dve":
            k.op("dve", lambda h: h.tensor_copy(out=dst_ap, in_=src_ap), reads=[bank], writes=[dst])
        else:
            k.op("act", lambda h: h.copy(out=dst_ap, in_=src_ap), reads=[bank], writes=[dst])

    def wload(l_ap, width):
        w = wring.next()
        for q4 in range(4):
            load(w, w.t[:, q4 * 4:(q4 + 1) * 4, 0:width], l_ap[q4 * 512:(q4 + 1) * 512, :].rearrange("(kc p) e -> p kc e", p=128), q="pool")
        return w

    def rope(src, s0, s1, i, dst, d0, d1, eng="dve"):
        rt = rt_r.next()
        c, s = cosT.t[:, i, :], sinT.t[:, i, :]
        k.op(eng, lambda h: h.tensor_tensor(out=rt.t[:, 0, :], in0=s0, in1=c, op=ALU.mult), reads=[src, cosT], writes=[rt])
        k.op(eng, lambda h: h.tensor_tensor(out=rt.t[:, 1, :], in0=s1, in1=s, op=ALU.mult), reads=[src, sinT], writes=[rt])
        k.op(eng, lambda h: h.tensor_tensor(out=rt.t[:, 2, :], in0=s0, in1=s, op=ALU.mult), reads=[src, sinT], writes=[rt])
        k.op(eng, lambda h: h.tensor_tensor(out=rt.t[:, 3, :], in0=s1, in1=c, op=ALU.mult), reads=[src, cosT], writes=[rt])
        k.op(eng, lambda h: h.tensor_tensor(out=d0, in0=rt.t[:, 0, :], in1=rt.t[:, 1, :], op=ALU.subtract), reads=[rt], writes=[dst])
        k.op(eng, lambda h: h.tensor_tensor(out=d1, in0=rt.t[:, 2, :], in1=rt.t[:, 3, :], op=ALU.add), reads=[rt], writes=[dst])

    def rstd(st, col, n, eps_t):
        k.op("act", lambda h: h.activation(out=st.t[:, col:col + 1], in_=st.t[:, col:col + 1], func=AF.Sqrt, bias=eps_t.t[:, 0:1], scale=1.0 / n),
             reads=[st, eps_t], writes=[st])
        k.op("dve", lambda h: h.reciprocal(out=st.t[:, col:col + 1], in_=st.t[:, col:col + 1]), reads=[st], writes=[st])

    def bcv(dst, dap, vec):
        load(dst, dap, vec.partition_broadcast(128))

    def params(l):
        k.enabled = PH("params")
        bcv(qg_bc, qg_bc.t[:, :], q_norm_g[l])
        bcv(kvg_bc, kvg_bc.t[:, :], kv_norm_g[l])
        bcv(sgg_bc, sgg_bc.t[:, :], sg_ln_g[l])
        bcv(sgb_bc, sgb_bc.t[:, :], sg_ln_b[l])
        load(sgbT, sgbT.t[:, :], sg_b[l].rearrange("g t -> t g"), allow_slow_non_contiguous=True)
        for g in range(4):
            xin = xin_r.next()
            load(xin, xin.t[:, 0:128], sg_w[l, g])
            k.op("dve", lambda h, xin=xin: h.tensor_tensor(out=xin.t[:, 0:128], in0=xin.t[:, 0:128], in1=sgmask.t[:, :], op=ALU.mult),
                 reads=[xin, sgmask], writes=[xin])
            xb = xb_r.next()
            k.op("dve", lambda h, xb=xb, xin=xin: h.tensor_copy(out=xb.t[:, 0:128], in_=xin.t[:, 0:128]), reads=[xin], writes=[xb])
            transposes(wspT, wspT.t[:, g, :], xb, [xb.t[:, 0:128]])
        memT = T(wsmall.t[:, 0:4096].rearrange("p (a b) -> p a b", a=16), share=wsmall)
        for mb in range(2):
            for pc in range(4):
                xin = xin_r.next()
                load(xin, xin.t[:, :], mem_d[mb * 128:(mb + 1) * 128, pc * 512:(pc + 1) * 512])
                xb = xb_r.next()
                k.op("dve", lambda h, xb=xb, xin=xin: h.tensor_copy(out=xb.t[:, :], in_=xin.t[:, :]), reads=[xin], writes=[xb])
                transposes(memT, memT.t[:, pc * 4:(pc + 1) * 4, mb * 128:(mb + 1) * 128], xb, [xb.t[:, j * 128:(j + 1) * 128] for j in range(4)])
        w = wload(w_mem_k[l], 256)
        for hh in range(4):
            bank = smring.next()
            for kc in range(16):
                k.op("pe", lambda h, kc=kc, hh=hh, bank=bank, w=w, memT=memT: h.matmul(bank.t[0:64, 0:256], w.t[:, kc, hh * 64:(hh + 1) * 64], memT.t[:, kc, :],
                                                                      start=(kc == 0), stop=(kc == 15)), reads=[w, memT], writes=[bank], inc=(kc == 15))
            k.op("dve", lambda h, hh=hh, bank=bank: h.tensor_copy(out=mkT.t[:, hh, :], in_=bank.t[0:64, 0:256]), reads=[bank], writes=[mkT])
        w = wload(w_mem_v[l], 256)
        for mb in range(2):
            bank = smring.next()
            for kc in range(16):
                k.op("pe", lambda h, kc=kc, mb=mb, bank=bank, w=w, memT=memT: h.matmul(bank.t[:, 0:256], memT.t[:, kc, mb * 128:(mb + 1) * 128], w.t[:, kc, 0:256],
                                                                      start=(kc == 0), stop=(kc == 15)), reads=[w, memT], writes=[bank], inc=(kc == 15))
            k.op("dve", lambda h, mb=mb, bank=bank: h.tensor_copy(out=mv.t[:, mb, :], in_=bank.t[:, 0:256]), reads=[bank], writes=[mv])


    for l in range(depth):
        xsrc = x_d if l == 0 else xs_d
        xsrc_T = [None] * 8 if l == 0 else xs_T

        if l == 0:
            params(0)
        k.enabled = PH("xT")
        for i in range(8):
            for pc in range(4):
                xin = xin_r.next()
                load(xin, xin.t[:, :], xsrc[i * 128:(i + 1) * 128, pc * 512:(pc + 1) * 512], src_T=xsrc_T[i])
                xb = xb_r.next()
                k.op("act", lambda h, xb=xb, xin=xin: h.copy(out=xb.t[:, :], in_=xin.t[:, :]), reads=[xin], writes=[xb])
                transposes(xT, xT.t[:, pc * 4:(pc + 1) * 4, i * 128:(i + 1) * 128], xb, [xb.t[:, j * 128:(j + 1) * 128] for j in range(4)],
                           evac=("dve" if pc % 2 == 0 else "act"))

        def inproj(c0, width, handler):
            pieces = [(0, width)]
            ws = [wload(w_in[l, :, c0 + p0:c0 + p0 + pw], pw) for (p0, pw) in pieces]
            pend = None
            for i in range(8):
                bank = mmring.next()
                for pi, (p0, pw) in enumerate(pieces):
                    for kc in range(16):
                        k.op("pe", lambda h, kc=kc, i=i, bank=bank, w=ws[pi], p0=p0, pw=pw: h.matmul(
                            bank.t[:, p0:p0 + pw], xT.t[:, kc, i * 128:(i + 1) * 128], w.t[:, kc, 0:pw], start=(kc == 0), stop=(kc == 15)),
                            reads=[xT, ws[pi]], writes=[bank], inc=(kc == 15 and pi == len(pieces) - 1))
                if pend is not None:
                    handler(*pend)
                pend = (i, bank)
            handler(*pend)

        k.enabled = PH("gates")
        for (c0, ycol, width) in GATE_CHUNKS:
            def gate(i, bank, ycol=ycol, width=width):
                k.op("act", lambda h: h.activation(out=ybuf[i].t[:, ycol:ycol + width], in_=bank.t[:, 0:width], func=AF.Silu),
                     reads=[bank], writes=[ybuf[i]])
            inproj(c0, width, gate)

        k.enabled = PH("kv")
        load(wsmall, wsmall.t[:, 0:3072].rearrange("p (a b) -> p a b", a=2), w_ukv[l].rearrange("(kc p) e -> p kc e", p=128), q="pool")

        def kv_handler(i, bank):
            ip = 7 - i
            hs = hseg_r.next()
            st = st_r.next()
            hb = hb_r.next()
            k.op("act", lambda h: h.copy(out=hs.t[:, 0:320], in_=bank.t[:, 0:320]), reads=[bank], writes=[hs])
            k.op("act", lambda h: h.activation(out=hb.t[:, 0:256], in_=hs.t[:, 0:256], func=AF.Square, accum_out=st.t[:, 0:1]),
                 reads=[hs], writes=[hb, st])
            rstd(st, 0, 256.0, eps6)
            k.op("dve", lambda h: h.scalar_tensor_tensor(out=hb.t[:, 0:256], in0=hs.t[:, 0:256], scalar=st.t[:, 0:1], in1=kvg_bc.t[:, :],
                                                         op0=ALU.mult, op1=ALU.mult), reads=[hs, st, kvg_bc], writes=[hb])
            transposes(kvnT, kvnT.t[:, :, :], hb, [hb.t[:, 0:128], hb.t[:, 128:256]])
            for n3 in range(3):
                bk = smring.next()
                for kc in range(2):
                    k.op("pe", lambda h, kc=kc, n3=n3, bk=bk: h.matmul(bk.t[:, :], kvnT.t[:, kc, :], wsmall.t[:, kc * 1536 + n3 * 512:kc * 1536 + (n3 + 1) * 512],
                                                                     start=(kc == 0), stop=(kc == 1)), reads=[kvnT, wsmall], writes=[bk], inc=(kc == 1))
                k.op("act", lambda h, n3=n3, bk=bk: h.copy(out=kvb.t[:, n3 * 512:(n3 + 1) * 512], in_=bk.t[:, :]), reads=[bk], writes=[kvb])
            for hp in range(2):
                ks = kst_r.next()
                nh = 4 if hp == 0 else 2
                transposes(ks, ks.t[:, 0:nh, :], kvb, [kvb.t[:, (hp * 4 + j) * 256:(hp * 4 + j) * 256 + 128] for j in range(nh)], rev=True)
                store(shk_T, shk_T.t[hp * 512:hp * 512 + nh * 128, ip * 128:(ip + 1) * 128].rearrange("(a p) c -> p a c", p=128), ks, ks.t[:, 0:nh, :])
            vs = vst_r.next()
            kv3 = kvb.t[:, :].rearrange("p (a b) -> p a b", b=256)
            for hp in range(2):
                bk = smring.next()
                k.op("pe", lambda h, hp=hp, bk=bk: h.matmul(bk.t[:, 0:384].rearrange("p (a b) -> p a b", a=3), jrev.t[:, :],
                                                            kv3[:, hp * 3:(hp + 1) * 3, 128:256], start=True, stop=True), reads=[kvb, jrev], writes=[bk])
                k.op("dve", lambda h, hp=hp, bk=bk: h.tensor_copy(out=vs.t[:, hp * 384:(hp + 1) * 384], in_=bk.t[:, 0:384]), reads=[bk], writes=[vs])
            store(shv_T, shv_ap[ip * 128:(ip + 1) * 128, 0:768], vs, vs.t[:, :])
            rope(hs, hs.t[:, 256:288], hs.t[:, 288:320], i, kpeb, kpeb.t[:, 0:32], kpeb.t[:, 32:64])
            transposes(kpst, kpst.t[:, :], kpeb, [kpeb.t[:, :]], rev=True, m=64)
            store(shk_T, shk_T.t[768:832, ip * 128:(ip + 1) * 128], kpst, kpst.t[:, :])
        inproj(512, 320, kv_handler)

        def sbk(i, bank):
            ip = 7 - i
            hb = hb_r.next()
            k.op("act", lambda h: h.copy(out=hb.t[:, :], in_=bank.t[:, :]), reads=[bank], writes=[hb])
            ks = kst_r.next()
            transposes(ks, ks.t[:, :, :], hb, [hb.t[:, j * 128:(j + 1) * 128] for j in range(4)], rev=True)
            store(shk_T, shk_T.t[832:1344, ip * 128:(ip + 1) * 128].rearrange("(a p) c -> p a c", p=128), ks, ks.t[:, :, :])
        inproj(3648, 512, sbk)

        def sbv(i, bank):
            ip = 7 - i
            hb = hb_r.next()
            k.op("act", lambda h: h.copy(out=hb.t[:, :], in_=bank.t[:, :]), reads=[bank], writes=[hb])
            bk = smring.next()
            k.op("pe", lambda h: h.matmul(bk.t[:, :], jrev.t[:, :], hb.t[:, :], start=True, stop=True), reads=[hb, jrev], writes=[bk])
            vs = vst_r.next()
            k.op("dve", lambda h: h.tensor_copy(out=vs.t[:, 0:512], in_=bk.t[:, :]), reads=[bk], writes=[vs])
            store(shv_T, shv_ap[ip * 128:(ip + 1) * 128, 768:1280], vs, vs.t[:, 0:512])
        inproj(4160, 512, sbv)

        k.enabled = PH("exch")
        k.op("pool", lambda h: h.collective_compute("AllGather", ALU.bypass, replica_groups=[list(range(NCORE))],
                                                    ins=[shk_T.t[:, :]], outs=[gak_T.t[:, :]]), reads=[shk_T], writes=[gak_T], sem=ccsem)

        k.enabled = PH("q")
        load(wsmall, wsmall.t[:, 0:4608].rearrange("p (a b) -> p a b", a=4), w_uq[l].rearrange("(kc p) e -> p kc e", p=128), q="pool")

        def cq_handler(i, bank):
            hs = hseg_r.next()
            st = st_r.next()
            hb = hb_r.next()
            k.op("act", lambda h: h.copy(out=hs.t[:, :], in_=bank.t[:, :]), reads=[bank], writes=[hs])
            k.op("act", lambda h: h.activation(out=hb.t[:, :], in_=hs.t[:, :], func=AF.Square, accum_out=st.t[:, 0:1]), reads=[hs], writes=[hb, st])
            rstd(st, 0, 512.0, eps6)
            k.op("dve", lambda h: h.scalar_tensor_tensor(out=hb.t[:, :], in0=hs.t[:, :], scalar=st.t[:, 0:1], in1=qg_bc.t[:, :],
                                                         op0=ALU.mult, op1=ALU.mult), reads=[hs, st, qg_bc], writes=[hb])
            transposes(cqnT, cqnT.t[:, :, :], hb, [hb.t[:, j * 128:(j + 1) * 128] for j in range(4)])
            for n3 in range(3):
                bk = smring.next()
                for kc in range(4):
                    k.op("pe", lambda h, kc=kc, n3=n3, bk=bk: h.matmul(bk.t[:, 0:384], cqnT.t[:, kc, :], wsmall.t[:, kc * 1152 + n3 * 384:kc * 1152 + (n3 + 1) * 384],
                                                                     start=(kc == 0), stop=(kc == 3)), reads=[cqnT, wsmall], writes=[bk], inc=(kc == 3))
                k.op("act", lambda h, n3=n3, bk=bk: h.activation(out=qb.t[:, n3 * 384:(n3 + 1) * 384], in_=bk.t[:, 0:384], func=AF.Copy, scale=MLA_SCALE),
                     reads=[bk], writes=[qb])
                k.op("dve", lambda h, n3=n3, bk=bk: h.tensor_scalar(out=qfr.t[:, 2 * n3:2 * n3 + 2, :],
                                                                   in0=bk.t[:, 0:384].rearrange("p (a b) -> p a b", a=2)[:, :, 128:192],
                                                                   scalar1=MLA_SCALE, scalar2=None, op0=ALU.mult), reads=[bk], writes=[qfr])
            for hh in range(6):
                b0 = hh * 192 + 128
                rope(qfr, qfr.t[:, hh, 0:32], qfr.t[:, hh, 32:64], i, qb, qb.t[:, b0:b0 + 32], qb.t[:, b0 + 32:b0 + 64],
                     eng="dve")
            for g4 in (0, 4):
                nh = 4 if g4 == 0 else 2
                transposes(qTn, qTn.t[:, g4:g4 + nh, i * 128:(i + 1) * 128], qb, [qb.t[:, (g4 + j) * 192:(g4 + j) * 192 + 128] for j in range(nh)])
                transposes(qTr, qTr.t[:, g4:g4 + nh, i * 128:(i + 1) * 128], qb, [qb.t[:, (g4 + j) * 192 + 128:(g4 + j) * 192 + 192] for j in range(nh)],
                           m=64, evac="act")
        inproj(0, 512, cq_handler)

        k.enabled = PH("sbq")
        def sbq(i, bank):
            hb = hb_r.next()
            k.op("act", lambda h: h.activation(out=hb.t[:, :], in_=bank.t[:, :], func=AF.Copy, scale=SB_SCALE), reads=[bank], writes=[hb])
            transposes(sbqT, sbqT.t[:, :, i * 128:(i + 1) * 128], hb, [hb.t[:, j * 128:(j + 1) * 128] for j in range(4)])
        inproj(3136, 512, sbq)

        k.enabled = PH("mem")
        def mqh(i, bank):
            hb = hb_r.next()
            st = st_r.next()
            k.op("act", lambda h: h.activation(out=hb.t[:, 0:256], in_=bank.t[:, 0:256], func=AF.Copy, scale=MEM_SCALE), reads=[bank], writes=[hb])
            transposes(mqT, mqT.t[:, :, :], hb, [hb.t[:, j * 64:(j + 1) * 64] for j in range(4)], m=64)
            obank = smring.next()
            sbanks = (smring.next(), smring.next())
            for hh in range(4):
                sbk_ = sbanks[hh // 2]
                sc0 = (hh % 2) * 256
                k.op("pe", lambda h, hh=hh, sbk_=sbk_, sc0=sc0: h.matmul(sbk_.t[:, sc0:sc0 + 256], mqT.t[:, hh, :], mkT.t[:, hh, :], start=True, stop=True),
                     reads=[mqT, mkT], writes=[sbk_])
                pm = pm_r.next()
                k.op("act", lambda h, hh=hh, sbk_=sbk_, pm=pm, sc0=sc0: h.activation(out=pm.t[:, :], in_=sbk_.t[:, sc0:sc0 + 256], func=AF.Exp,
                                                                                     accum_out=st.t[:, hh:hh + 1]),
                     reads=[sbk_], writes=[pm, st])
                pmT = pmT_r.next()
                transposes(pmT, pmT.t[:, :, :], pm, [pm.t[:, 0:128], pm.t[:, 128:256]])
                for mb in range(2):
                    k.op("pe", lambda h, hh=hh, mb=mb, pmT=pmT: h.matmul(obank.t[:, hh * 64:(hh + 1) * 64], pmT.t[:, mb, :], mv.t[:, mb, hh * 64:(hh + 1) * 64],
                                                                         start=(mb == 0), stop=(mb == 1)), reads=[pmT, mv], writes=[obank], inc=(mb == 1))
            k.op("dve", lambda h: h.reciprocal(out=st.t[:, 4:8], in_=st.t[:, 0:4]), reads=[st], writes=[st])
            for hh in range(4):
                c0 = 1792 + hh * 64
                k.op("dve", lambda h, hh=hh, c0=c0: h.scalar_tensor_tensor(out=ybuf[i].t[:, c0:c0 + 64], in0=obank.t[:, hh * 64:(hh + 1) * 64],
                                                                           scalar=st.t[:, 4 + hh:5 + hh], in1=ybuf[i].t[:, c0:c0 + 64],
                                                                           op0=ALU.mult, op1=ALU.mult), reads=[obank, st, ybuf[i]], writes=[ybuf[i]])
        inproj(5184, 256, mqh)

        k.enabled = PH("gmlp")
        def sgv(i, bank):
            hs = hseg_r.next()
            st = st_r.next()
            hb = hb_r.next()
            k.op("act", lambda h: h.activation(out=hs.t[:, :], in_=bank.t[:, :], func=AF.Gelu_apprx_tanh, accum_out=st.t[:, 0:1]), reads=[bank], writes=[hs, st])
            k.op("dve", lambda h: h.tensor_scalar(out=st.t[:, 0:1], in0=st.t[:, 0:1], scalar1=1.0 / 512.0, scalar2=None, op0=ALU.mult), reads=[st], writes=[st])
            k.op("dve", lambda h: h.tensor_scalar(out=hs.t[:, :], in0=hs.t[:, :], scalar1=st.t[:, 0:1], scalar2=None, op0=ALU.subtract), reads=[hs, st], writes=[hs])
            k.op("act", lambda h: h.activation(out=hb.t[:, :], in_=hs.t[:, :], func=AF.Square, accum_out=st.t[:, 1:2]), reads=[hs], writes=[hb, st])
            rstd(st, 1, 512.0, eps5)
            k.op("dve", lambda h: h.scalar_tensor_tensor(out=hs.t[:, :], in0=hs.t[:, :], scalar=st.t[:, 1:2], in1=sgg_bc.t[:, :], op0=ALU.mult, op1=ALU.mult),
                 reads=[hs, st, sgg_bc], writes=[hs])
            k.op("dve", lambda h: h.tensor_tensor(out=hb.t[:, :], in0=hs.t[:, :], in1=sgb_bc.t[:, :], op=ALU.add), reads=[hs, sgb_bc], writes=[hb])
            bk = smring.next()
            for g in range(4):
                k.op("pe", lambda h, g=g: h.matmul(bk.t[:, g * 128:(g + 1) * 128], wspT.t[:, g, :], hb.t[:, g * 128:(g + 1) * 128], start=True, stop=True),
                     reads=[wspT, hb], writes=[bk], inc=(g == 3))
            for g in range(4):
                c0 = 768 + g * 128
                k.op("dve", lambda h, g=g, c0=c0: h.scalar_tensor_tensor(out=ybuf[i].t[:, c0:c0 + 128], in0=bk.t[:, g * 128:(g + 1) * 128], scalar=sgbT.t[:, g:g + 1],
                                                                         in1=ybuf[i].t[:, c0:c0 + 128], op0=ALU.add, op1=ALU.mult),
                     reads=[bk, sgbT, ybuf[i]], writes=[ybuf[i]])
        inproj(2112, 512, sgv)

        def sgu(i, bank):
            hb = hb_r.next()
            k.op("act", lambda h: h.activation(out=hb.t[:, :], in_=bank.t[:, :], func=AF.Gelu_apprx_tanh), reads=[bank], writes=[hb])
            k.op("dve", lambda h: h.tensor_tensor(out=ybuf[i].t[:, 768:1280], in0=ybuf[i].t[:, 768:1280], in1=hb.t[:, :], op=ALU.mult),
                 reads=[hb, ybuf[i]], writes=[ybuf[i]])
        inproj(1600, 512, sgu)

        k.enabled = PH("attn")
        gak3 = gak_T.t.rearrange("(r w) c -> w r c", r=NCORE)
        gav4 = gak_T.t.rearrange("a c -> (a c)").rearrange("(r z) -> r z", r=NCORE)[:, KROWS * 1024:SHROWS * 1024].rearrange(
            "r (i p c) -> p r i c", i=8, p=128)

        class HC:
            pass
        heads = []
        for hh in range(6):
            heads.append(("mla", hh))
            if hh < 4:
                heads.append(("sb", hh))
        chunks, steps = [], []
        for (kind, hh) in heads:
            hc = HC()
            hc.kind, hc.hh = kind, hh
            hc.accs, hc.rs, hc.cry, hc.started = accsets.next(), rs_r.next(), cry_r.next(), set()
            if kind == "mla":
                hc.hrow, hc.hcol, hc.ycol = hh * 128, hh * 128, hh * 128
            else:
                hc.hrow, hc.hcol, hc.ycol = 832 + hh * 128, 768 + hh * 128, 1280 + hh * 128
            for cc in range(16):
                ch = HC()
                ch.hc, ch.cc, ch.loaded = hc, cc, False
                chunks.append(ch)
                for i in range(max(0, 7 - cc // 2), 8):
                    st_ = HC()
                    st_.ch, st_.i, st_.ci = ch, i, len(chunks) - 1
                    st_.first_of_head = (cc == 0 and i == 7)
                    st_.last_of_head = (cc == 15 and i == 7)
                    steps.append(st_)

        def load_chunk(ci):
            ch = chunks[ci]
            if ch.loaded:
                return
            ch.loaded = True
            hc, cc = ch.hc, ch.cc
            ipb, r0 = cc // 2, 4 * (cc % 2)
            ch.kc = kc_r.items[ci % 3]
            ch.vc = vc_r.items[ci % 3]
            load(ch.kc, ch.kc.t[:, :, :], gak3[hc.hrow:hc.hrow + 128, r0:r0 + 4, ipb * 128:(ipb + 1) * 128], src_T=gak_T)
            load(ch.vc, ch.vc.t[:, :, :], gav4[:, r0:r0 + 4, ipb, hc.hcol:hc.hcol + 128], src_T=gav_T)
            if hc.kind == "mla":
                ch.kr = kr_r.items[ci % 3]
                load(ch.kr, ch.kr.t[:, :, :], gak3[768:832, r0:r0 + 4, ipb * 128:(ipb + 1) * 128], src_T=gak_T)

        def S1(sp):
            ch, i = sp.ch, sp.i
            hc, cc = ch.hc, ch.cc
            hh, rs, cry = hc.hh, hc.rs, hc.cry
            if sp.first_of_head:
                if hc.kind == "mla":
                    k.op("dve", lambda h: h.memset(rs.t[:, :, :], 0.0), writes=[rs])
                else:
                    k.op("dve", lambda h: h.memset(cry.t[:, :], 1.0), writes=[cry])
            step = cc - (14 - 2 * i)
            zone = step in (0, 1)
            zc = step * 512
            sbank = mmring.next()
            qs = slice(i * 128, (i + 1) * 128)
            p = p_r.next()
            sp.p = p
            kc_, kflat = ch.kc, ch.kc.t[:, :, :].rearrange("p a b -> p (a b)")
            if hc.kind == "mla":
                kr_ = ch.kr
                k.op("pe", lambda h: h.matmul(sbank.t[:, :], qTn.t[:, hh, qs], kflat, start=True, stop=False),
                     reads=[qTn, kc_], writes=[sbank], inc=False)
                k.op("pe", lambda h: h.matmul(sbank.t[:, :], qTr.t[:, hh, qs], kr_.t[:, :, :].rearrange("p a b -> p (a b)"),
                                              start=False, stop=True), reads=[qTr, kr_], writes=[sbank])
                if not zone:
                    k.op("act", lambda h: h.activation(out=p.t[:, :], in_=sbank.t[:, :], func=AF.Exp, accum_out=rs.t[:, i, step:step + 1]),
                         reads=[sbank], writes=[p, rs])
                else:
                    pf = pf_r.next()
                    k.op("act", lambda h: h.activation(out=pf.t[:, :], in_=sbank.t[:, :], func=AF.Exp), reads=[sbank], writes=[pf])
                    k.op("dve", lambda h: h.scalar_tensor_tensor(out=p.t[:, :], in0=pf.t[:, :], scalar=1.0, in1=mmask.t[:, zc:zc + 512],
                                                                 op0=ALU.mult, op1=ALU.mult, accum_out=rs.t[:, i, step:step + 1]),
                         reads=[pf, mmask], writes=[p, rs])
            else:
                k.op("pe", lambda h: h.matmul(sbank.t[:, :], sbqT.t[:, hh, qs], kflat, start=True, stop=True), reads=[sbqT, kc_], writes=[sbank])
                gb = g_r.next()
                k.op("act", lambda h: h.activation(out=gb.t[:, :], in_=sbank.t[:, :], func=AF.Sigmoid, scale=-1.0), reads=[sbank], writes=[gb])
                if zone:
                    k.op("dve", lambda h: h.tensor_tensor(out=gb.t[:, :], in0=gb.t[:, :], in1=sbm1.t[:, zc:zc + 512], op=ALU.max),
                         reads=[gb, sbm1], writes=[gb])
                eb = e_r.next()
                k.op("dve", lambda h: h.tensor_copy(out=eb.t[:, 0:1], in_=cry.t[:, i:i + 1]), reads=[cry], writes=[eb])
                k.op("dve", lambda h: h.tensor_tensor_scan(out=eb.t[:, 1:513], data0=gb.t[:, :], data1=zeros.t[:, :], initial=cry.t[:, i:i + 1],
                                                           op0=ALU.mult, op1=ALU.add), reads=[gb, zeros, cry], writes=[eb])
                k.op("dve", lambda h: h.tensor_copy(out=cry.t[:, i:i + 1], in_=eb.t[:, 512:513]), reads=[eb], writes=[cry])
                k.op("dve", lambda h: h.tensor_tensor(out=p.t[:, :], in0=eb.t[:, 0:512], in1=eb.t[:, 1:513], op=ALU.subtract), reads=[eb], writes=[p])

        def S2(sp):
            p = sp.p
            sp.pt = pt_r.next()
            transposes(sp.pt, sp.pt.t[:, :, :], p, [p.t[:, b * 128:(b + 1) * 128] for b in range(4)],
                       evac=("dve" if sp.ch.hc.kind == "mla" else "act"))

        def S3(sp):
            ch, i, pt = sp.ch, sp.i, sp.pt
            hc, cc, vc_ = ch.hc, ch.cc, ch.vc
            ab = hc.accs[i // 4]
            for b in range(4):
                st_flag = (i // 4) not in hc.started
                hc.started.add(i // 4)
                k.op("pe", lambda h, b=b, st_flag=st_flag: h.matmul(ab.t[:, (i % 4) * 128:(i % 4 + 1) * 128], pt.t[:, b, :], vc_.t[:, b, :],
                                                                    start=st_flag, stop=(cc == 15 and b == 3), skip_group_check=True),
                     reads=[pt, vc_], writes=[ab], inc=(b == 3))
            if sp.last_of_head:
                ycol, rs = hc.ycol, hc.rs
                if hc.kind == "mla":
                    k.op("dve", lambda h: h.reduce_sum(out=rsum.t[:, :], in_=rs.t[:, :, :], axis=AX.X), reads=[rs], writes=[rsum])
                    k.op("dve", lambda h: h.reciprocal(out=rsum.t[:, :], in_=rsum.t[:, :]), reads=[rsum], writes=[rsum])
                for i2 in range(8):
                    ab2 = hc.accs[i2 // 4]
                    aap = ab2.t[:, (i2 % 4) * 128:(i2 % 4 + 1) * 128]
                    if hc.kind == "mla":
                        k.op("dve", lambda h, i2=i2, aap=aap: h.scalar_tensor_tensor(out=ybuf[i2].t[:, ycol:ycol + 128], in0=aap, scalar=rsum.t[:, i2:i2 + 1],
                                                                                     in1=ybuf[i2].t[:, ycol:ycol + 128], op0=ALU.mult, op1=ALU.mult),
                             reads=[ab2, rsum, ybuf[i2]], writes=[ybuf[i2]])
                    else:
                        k.op("dve", lambda h, i2=i2, aap=aap: h.tensor_tensor(out=ybuf[i2].t[:, ycol:ycol + 128], in0=aap, in1=ybuf[i2].t[:, ycol:ycol + 128],
                                                                              op=ALU.mult), reads=[ab2, ybuf[i2]], writes=[ybuf[i2]])

        NS = len(steps)
        for it in range(NS + 2):
            base = steps[min(max(it - 2, 0), NS - 1)].ci
            for ci in range(base, min(base + 3, len(chunks))):
                load_chunk(ci)
            if it < NS:
                S1(steps[it])
            if 1 <= it <= NS:
                S2(steps[it - 1])
            if it >= 2:
                S3(steps[it - 2])

        k.enabled = True
        if dbg and l == depth - 1:
            for i in range(8):
                store(ydbg_T, ydbg_d[i * 128:(i + 1) * 128, :], ybuf[i], ybuf[i].t[:, :])

        if l + 1 < depth:
            params(l + 1)
        k.enabled = PH("out")
        for i in range(8):
            for g4 in range(4):
                transposes(xT, xT.t[:, g4 * 4:(g4 + 1) * 4, i * 128:(i + 1) * 128], ybuf[i],
                           [ybuf[i].t[:, (g4 * 4 + j) * 128:(g4 * 4 + j + 1) * 128] for j in range(4)], evac=("dve" if g4 % 2 == 0 else "act"))
        for c4 in range(4):
          w = wload(w_out[l, :, c4 * 512:(c4 + 1) * 512], 512)
          for i in range(8):
            bank = mmring.next()
            for kc in range(16):
                k.op("pe", lambda h, kc=kc, i=i, bank=bank, w=w: h.matmul(bank.t[:, :], xT.t[:, kc, i * 128:(i + 1) * 128], w.t[:, kc, :],
                                                                      start=(kc == 0), stop=(kc == 15)), reads=[xT, w], writes=[bank], inc=(kc == 15))
            for half in range(2):
                c8 = c4 * 2 + half
                cs = slice(c8 * 256, (c8 + 1) * 256)
                bo = half * 256
                zp = zp_r.next()
                load(zp, zp.t[:, :], xsrc[i * 128:(i + 1) * 128, cs], src_T=xsrc_T[i])
                k.op("dve", lambda h, i=i, bank=bank, zp=zp, c8=c8, bo=bo: h.scalar_tensor_tensor(out=zp.t[:, :], in0=zp.t[:, :], scalar=ALPHA, in1=bank.t[:, bo:bo + 256],
                                                                                         op0=ALU.mult, op1=ALU.add, accum_out=lnsum.t[:, i, c8:c8 + 1]),
                     reads=[bank, zp], writes=[zp, lnsum])
                zq = bf_r.next()
                k.op("act", lambda h, i=i, zp=zp, zq=zq, c8=c8: h.activation(out=zq.t[:, 0:256], in_=zp.t[:, :], func=AF.Square, accum_out=lnsq.t[:, i, c8:c8 + 1]),
                     reads=[zp], writes=[zq, lnsq])
                store(xs_T[i], xs_d[i * 128:(i + 1) * 128, cs], zp, zp.t[:, :])
        k.op("dve", lambda h: h.reduce_sum(out=lnst.t[:, :, 0], in_=lnsum.t[:, :, :], axis=AX.X), reads=[lnsum], writes=[lnst])
        k.op("dve", lambda h: h.reduce_sum(out=lnst.t[:, :, 1], in_=lnsq.t[:, :, :], axis=AX.X), reads=[lnsq], writes=[lnst])
        k.op("dve", lambda h: h.tensor_scalar(out=lnst.t[:, :, 0:2], in0=lnst.t[:, :, 0:2], scalar1=1.0 / D, scalar2=None, op0=ALU.mult), reads=[lnst], writes=[lnst])
        k.op("dve", lambda h: h.tensor_tensor(out=lnst.t[:, :, 2], in0=lnst.t[:, :, 0], in1=lnst.t[:, :, 0], op=ALU.mult), reads=[lnst], writes=[lnst])
        k.op("dve", lambda h: h.tensor_tensor(out=lnst.t[:, :, 1], in0=lnst.t[:, :, 1], in1=lnst.t[:, :, 2], op=ALU.subtract), reads=[lnst], writes=[lnst])
        k.op("act", lambda h: h.activation(out=lnst.t[:, :, 1], in_=lnst.t[:, :, 1], func=AF.Sqrt, bias=eps5.t[:, 0:1], scale=1.0), reads=[lnst, eps5], writes=[lnst])
        k.op("dve", lambda h: h.reciprocal(out=lnst.t[:, :, 1], in_=lnst.t[:, :, 1]), reads=[lnst], writes=[lnst])
        for c8 in range(8):
            cs = slice(c8 * 256, (c8 + 1) * 256)
            lbt = f32_r.next()
            lb = T(lbt.t[:, :].rearrange("p (a b) -> p a b", a=2), share=lbt)
            lb.sem = tsem(lbt)
            load(lb, lb.t[:, 0, :], ln_g[l][cs].partition_broadcast(128))
            load(lb, lb.t[:, 1, :], ln_b[l][cs].partition_broadcast(128))
            for i in range(8):
                zp = zp_r.next()
                load(zp, zp.t[:, :], xs_d[i * 128:(i + 1) * 128, cs], src_T=xs_T[i])
                k.op("dve", lambda h, i=i, zp=zp: h.tensor_scalar(out=zp.t[:, :], in0=zp.t[:, :], scalar1=lnst.t[:, i, 0:1], scalar2=lnst.t[:, i, 1:2],
                                                                op0=ALU.subtract, op1=ALU.mult), reads=[zp, lnst], writes=[zp])
                k.op("pool", lambda h, zp=zp, lb=lb: h.tensor_tensor(out=zp.t[:, :], in0=zp.t[:, :], in1=lb.t[:, 0, :], op=ALU.mult), reads=[zp, lb], writes=[zp])
                k.op("pool", lambda h, zp=zp, lb=lb: h.tensor_tensor(out=zp.t[:, :], in0=zp.t[:, :], in1=lb.t[:, 1, :], op=ALU.add), reads=[zp, lb], writes=[zp])
                if l == depth - 1:
                    store(out_T[i], out_d[i * 128:(i + 1) * 128, cs], zp, zp.t[:, :])
                else:
                    store(xs_T[i], xs_d[i * 128:(i + 1) * 128, cs], zp, zp.t[:, :])
    k.enabled = True
    fin = out_T + ([ydbg_T, dg_T, dp_T, dk_T, dq_T, de_T] if dbg else [])
    k.op("sp", lambda h: h.nop(), reads=fin)
    k.emit()
    return nc


def _host_consts(c):
    ident = np.eye(128, dtype=np.float32)
    jrev = ident[::-1].copy()
    zc = np.arange(1024)
    z, pp = zc // 128, zc % 128
    kpos = (7 - z) * 128 + (127 - pp)
    qpos = (7 - c) * 128 + np.arange(128)
    mmask = (kpos[None, :] // 64 <= qpos[:, None] // 64).astype(np.float32)
    sbm1 = (kpos[None, :] >= qpos[:, None]).astype(np.float32)
    inv_freq = (np.float32(10000.0) ** (-np.arange(0, 64, 2, dtype=np.float32) / np.float32(64.0))).astype(np.float32)
    pin = np.arange(128) // 64
    sgmask = (pin[None, :] <= pin[:, None]).astype(np.float32)
    return {
        "ident": ident.astype(ml_dtypes.bfloat16), "jrev": jrev.astype(ml_dtypes.bfloat16),
        "mmask": mmask.astype(ml_dtypes.bfloat16), "sbm1": sbm1,
        "invf": np.broadcast_to(inv_freq[None, :], (128, 32)).copy(), "sgmask": sgmask,
    }


_NC_CACHE = {}


def kernel(x, mem, positions, w_in, q_norm_g, w_uq, kv_norm_g, w_ukv, sg_ln_g, sg_ln_b, sg_w, sg_b,
           w_mem_k, w_mem_v, w_out, ln_g, ln_b, _depth=DEPTH, _dbg=False):
    f = lambda a: np.ascontiguousarray(np.asarray(a, dtype=np.float32))
    x2 = f(x)[0]
    pos = np.asarray(positions)[0].astype(np.int32)
    shared = {"mem": f(mem)[0], "w_in": f(w_in), "q_norm_g": f(q_norm_g), "w_uq": f(w_uq), "kv_norm_g": f(kv_norm_g),
              "w_ukv": f(w_ukv), "sg_ln_g": f(sg_ln_g), "sg_ln_b": f(sg_ln_b), "sg_w": f(sg_w), "sg_b": f(sg_b),
              "w_mem_k": f(w_mem_k), "w_mem_v": f(w_mem_v), "w_out": f(w_out), "ln_g": f(ln_g), "ln_b": f(ln_b)}
    in_maps = []
    for c in range(NCORE):
        blocks = [8 * i + 7 - c for i in range(8)]
        xc = np.concatenate([x2[g * 128:(g + 1) * 128] for g in blocks], axis=0)
        pc = np.stack([pos[g * 128:(g + 1) * 128] for g in blocks], axis=1)
        m = {"x": np.ascontiguousarray(xc), "pos": np.ascontiguousarray(pc)}
        m.update(shared)
        m.update(_host_consts(c))
        in_maps.append(m)
    key = (_depth, _dbg)
    if key not in _NC_CACHE:
        _NC_CACHE[key] = build(_depth, _dbg)
    nc = _NC_CACHE[key]
    res = run_bass_kernel_spmd(nc, in_maps, core_ids=list(range(NCORE)))
    out = np.empty((1, 8192, D), np.float32)
    ydbg = np.empty((8192, D), np.float32) if _dbg else None
    for c in range(NCORE):
        r = res.results[c]
        for i in range(8):
            g = 8 * i + 7 - c
            out[0, g * 128:(g + 1) * 128] = r["out"][i * 128:(i + 1) * 128]
            if _dbg:
                ydbg[g * 128:(g + 1) * 128] = np.asarray(r["ydbg"][i * 128:(i + 1) * 128], dtype=np.float32)
    if _dbg:
        return out, ydbg
    return out
```
